# Optimizing a Trainium2 kernel written in Bass

```python
import math
import jax, jax.numpy as jnp
from jax import lax
import numpy as np

D_MODEL = 1024
BATCH = 4
SEQ = 4096
DEPTH = 1

DA_HEADS = 4
DA_QK_DIM = 64
DA_V_DIM = 2 * DA_QK_DIM
SB_HEADS = 4
SB_HEAD_DIM = 128
D_FF = 2816
Q_BLOCK = 128
NORM_EPS = 1e-5
N_BRANCHES = 2
DA_QK_WIDTH = DA_HEADS * 2 * DA_QK_DIM
DA_V_WIDTH = DA_HEADS * DA_V_DIM
SB_WIDTH = SB_HEADS * SB_HEAD_DIM
IN_SPLITS = (DA_QK_WIDTH, DA_QK_WIDTH, DA_V_WIDTH, SB_WIDTH, SB_WIDTH, SB_WIDTH, N_BRANCHES * D_MODEL)
D_IN = sum(IN_SPLITS)
IN_SPLIT_POINTS = [int(v) for v in np.cumsum(IN_SPLITS)[:-1]]

kernel_name = "hybrid_gated_diffattn_stickbreak_macaron"


def rmsnorm(x, g):
    xf = x.astype(jnp.float32)
    y = xf * lax.rsqrt(jnp.mean(xf * xf, axis=-1, keepdims=True) + NORM_EPS)
    return (y * g.astype(jnp.float32)).astype(x.dtype)


def swiglu(x, w_gate, w_up, w_down):
    return (jax.nn.silu(x @ w_gate) * (x @ w_up)) @ w_down


def alibi_slopes(n_heads):
    return jnp.exp2(-8.0 * jnp.arange(1, n_heads + 1, dtype=jnp.float32) / n_heads)


def lambda_init_fn(layer_idx):
    return 0.8 - 0.6 * math.exp(-0.3 * layer_idx)


def diff_attention(q, k, v, lam):
    B, H, _, S, _ = q.shape
    dv = v.shape[-1]
    n_blocks = S // Q_BLOCK
    scale = DA_QK_DIM ** -0.5
    slopes = alibi_slopes(H)[:, None, None, None]
    k_pos = jnp.arange(S)

    def block(i):
        start = i * Q_BLOCK
        qb = lax.dynamic_slice_in_dim(q, start, Q_BLOCK, axis=3)
        q_pos = start + jnp.arange(Q_BLOCK)
        dist = q_pos[:, None] - k_pos[None, :]
        causal = dist >= 0
        s = jnp.einsum('bhmqd,bhmkd->bhmqk', qb, k).astype(jnp.float32) * scale
        s = s - slopes * dist.astype(jnp.float32)
        s = jnp.where(causal, s, -jnp.inf)
        p = jax.nn.softmax(s, axis=-1)
        w = p[:, :, 0] - lam * p[:, :, 1]
        return jnp.einsum('bhqk,bhkd->bhqd', w.astype(v.dtype), v)

    out = lax.map(block, jnp.arange(n_blocks))
    return jnp.moveaxis(out, 0, 2).reshape(B, H, S, dv)


def stick_breaking_attention(q, k, v):
    B, H, S, d = q.shape
    n_blocks = S // Q_BLOCK
    scale = d ** -0.5
    k_pos = jnp.arange(S)

    def block(i):
        start = i * Q_BLOCK
        qb = lax.dynamic_slice_in_dim(q, start, Q_BLOCK, axis=2)
        q_pos = start + jnp.arange(Q_BLOCK)
        strict = k_pos[None, :] < q_pos[:, None]
        z = jnp.einsum('bhqd,bhkd->bhqk', qb, k).astype(jnp.float32) * scale
        u = jnp.where(strict, jax.nn.softplus(z), 0.0)
        tail = lax.cumsum(u, axis=3, reverse=True) - u
        log_a = jax.nn.log_sigmoid(z) - tail
        a = jnp.where(strict, jnp.exp(log_a), 0.0)
        return jnp.einsum('bhqk,bhkd->bhqd', a.astype(v.dtype), v)

    out = lax.map(block, jnp.arange(n_blocks))
    return jnp.moveaxis(out, 0, 2).reshape(B, H, S, d)


def gated_mixer(xn, w_in, b_gate, lambda_q1, lambda_k1, lambda_q2, lambda_k2,
                diff_subln, w_branch_diff, w_branch_sb, w_out, lam_init):
    B, S, _ = xn.shape
    proj = xn @ w_in
    da_q, da_k, da_v, sb_q, sb_k, sb_v, gate_logits = jnp.split(proj, IN_SPLIT_POINTS, axis=-1)

    da_q = da_q.reshape(B, S, DA_HEADS, 2, DA_QK_DIM).transpose(0, 2, 3, 1, 4)
    da_k = da_k.reshape(B, S, DA_HEADS, 2, DA_QK_DIM).transpose(0, 2, 3, 1, 4)
    da_v = da_v.reshape(B, S, DA_HEADS, DA_V_DIM).transpose(0, 2, 1, 3)
    lam = (jnp.exp(jnp.sum(lambda_q1.astype(jnp.float32) * lambda_k1.astype(jnp.float32)))
           - jnp.exp(jnp.sum(lambda_q2.astype(jnp.float32) * lambda_k2.astype(jnp.float32)))
           + lam_init)
    a = diff_attention(da_q, da_k, da_v, lam)
    a = rmsnorm(a, diff_subln) * (1.0 - lam_init)
    a = a.transpose(0, 2, 1, 3).reshape(B, S, DA_V_WIDTH)

    sb_q = sb_q.reshape(B, S, SB_HEADS, SB_HEAD_DIM).transpose(0, 2, 1, 3)
    sb_k = sb_k.reshape(B, S, SB_HEADS, SB_HEAD_DIM).transpose(0, 2, 1, 3)
    sb_v = sb_v.reshape(B, S, SB_HEADS, SB_HEAD_DIM).transpose(0, 2, 1, 3)
    b = stick_breaking_attention(sb_q, sb_k, sb_v)
    b = b.transpose(0, 2, 1, 3).reshape(B, S, SB_WIDTH)

    gates = jax.nn.sigmoid(gate_logits + b_gate).reshape(B, S, N_BRANCHES, D_MODEL)
    y = gates[:, :, 0] * (a @ w_branch_diff) + gates[:, :, 1] * (b @ w_branch_sb)
    return y @ w_out


def setup_inputs(seed: int = 0) -> dict:
    key = jax.random.key(seed)
    ks = jax.random.split(key, 24)
    f32 = jnp.float32
    L = DEPTH

    def w(k, shape, fan_in):
        return jax.random.normal(k, shape, f32) * fan_in ** -0.5

    def gain(k, shape):
        return 1.0 + 0.01 * jax.random.normal(k, shape, f32)

    return {
        "x": jax.random.normal(ks[0], (BATCH, SEQ, D_MODEL), f32),
        "ffn1_norm": gain(ks[1], (L, D_MODEL)),
        "ffn1_w_gate": w(ks[2], (L, D_MODEL, D_FF), D_MODEL),
        "ffn1_w_up": w(ks[3], (L, D_MODEL, D_FF), D_MODEL),
        "ffn1_w_down": w(ks[4], (L, D_FF, D_MODEL), D_FF),
        "mix_norm": gain(ks[5], (L, D_MODEL)),
        "w_in": w(ks[6], (L, D_MODEL, D_IN), D_MODEL),
        "b_gate": 0.01 * jax.random.normal(ks[7], (L, N_BRANCHES * D_MODEL), f32),
        "lambda_q1": 0.1 * jax.random.normal(ks[8], (L, DA_QK_DIM), f32),
        "lambda_k1": 0.1 * jax.random.normal(ks[9], (L, DA_QK_DIM), f32),
        "lambda_q2": 0.1 * jax.random.normal(ks[10], (L, DA_QK_DIM), f32),
        "lambda_k2": 0.1 * jax.random.normal(ks[11], (L, DA_QK_DIM), f32),
        "diff_subln": gain(ks[12], (L, DA_V_DIM)),
        "w_branch_diff": w(ks[13], (L, DA_V_WIDTH, D_MODEL), DA_V_WIDTH),
        "w_branch_sb": w(ks[14], (L, SB_WIDTH, D_MODEL), SB_WIDTH),
        "w_out": w(ks[15], (L, D_MODEL, D_MODEL), D_MODEL),
        "ffn2_norm": gain(ks[16], (L, D_MODEL)),
        "ffn2_w_gate": w(ks[17], (L, D_MODEL, D_FF), D_MODEL),
        "ffn2_w_up": w(ks[18], (L, D_MODEL, D_FF), D_MODEL),
        "ffn2_w_down": w(ks[19], (L, D_FF, D_MODEL), D_FF),
        "final_norm": gain(ks[20], (D_MODEL,)),
    }


def reference(x, ffn1_norm, ffn1_w_gate, ffn1_w_up, ffn1_w_down, mix_norm, w_in, b_gate,
              lambda_q1, lambda_k1, lambda_q2, lambda_k2, diff_subln, w_branch_diff,
              w_branch_sb, w_out, ffn2_norm, ffn2_w_gate, ffn2_w_up, ffn2_w_down, final_norm):
    h = x
    for l in range(DEPTH):
        lam_init = lambda_init_fn(l)
        h = h + 0.5 * swiglu(rmsnorm(h, ffn1_norm[l]), ffn1_w_gate[l], ffn1_w_up[l], ffn1_w_down[l])
        h = h + gated_mixer(rmsnorm(h, mix_norm[l]), w_in[l], b_gate[l],
                            lambda_q1[l], lambda_k1[l], lambda_q2[l], lambda_k2[l],
                            diff_subln[l], w_branch_diff[l], w_branch_sb[l], w_out[l], lam_init)
        h = h + 0.5 * swiglu(rmsnorm(h, ffn2_norm[l]), ffn2_w_gate[l], ffn2_w_up[l], ffn2_w_down[l])
    return rmsnorm(h, final_norm)
```

```python
import os
import numpy as np
import ml_dtypes
import concourse.bass as bass
import concourse.mybir as mybir
from concourse.bass_utils import run_bass_kernel_spmd

F32 = mybir.dt.float32
BF16 = mybir.dt.bfloat16
AF = mybir.ActivationFunctionType
ALU = mybir.AluOpType

ENGS = ("pe", "act", "dve", "pool", "sp")
D = 1024
DFF = 2816
S = 4096
T = 2048
NTB = 16
NTT = 4
PARTS = [(0, 6), (6, 6), (12, 5), (17, 5)]
EPS = 1e-5
LAM_INIT = 0.8 - 0.6 * 1.0
PAIRS = [[0, 1], [2, 3], [4, 5], [6, 7]]
NJ = 32


def _freeze(fn):
    import types
    if fn is None or fn.__closure__ is None:
        return fn
    cells = []
    for c in fn.__closure__:
        try:
            cells.append(types.CellType(c.cell_contents))
        except ValueError:
            cells.append(c)
    return types.FunctionType(fn.__code__, fn.__globals__, fn.__name__, fn.__defaults__, tuple(cells))


class Prog:
    def __init__(self):
        self.ops = []
        self.eng_ops = {e: [] for e in ENGS}
        self.lastw = {}
        self.readers = {}
        self.sem_counts = {}
        self.sem_inc = {}
        self.last_c = {}
        self.dma_since = []

    def _collect(self, reads, writes, tok):
        deps = []
        for r in reads:
            t = self.lastw.get(r)
            if t is not None:
                deps.append(t)
        for w in writes:
            t = self.lastw.get(w)
            if t is not None:
                deps.append(t)
            deps.extend(self.readers.get(w, ()))
        for w in writes:
            self.lastw[w] = tok
            self.readers[w] = []
        for r in reads:
            if r in writes:
                continue
            self.readers.setdefault(r, []).append(tok)
        return [d for d in deps if d != tok]

    def op(self, eng, fn, reads=(), writes=()):
        idx = len(self.eng_ops[eng])
        tok = ("c", eng, idx)
        deps = self._collect(tuple(reads), tuple(writes), tok)
        o = dict(eng=eng, fn=_freeze(fn), deps=deps, kind="c", tok=tok)
        self.eng_ops[eng].append(o)
        self.ops.append(o)
        self.last_c[eng] = tok
        return tok

    def dma(self, eng, fn, reads=(), writes=(), sem=None, inc=16):
        n = self.sem_counts.get(sem, 0) + 1
        self.sem_counts[sem] = n
        self.sem_inc[sem] = inc
        tok = ("d", sem, n)
        deps = self._collect(tuple(reads), tuple(writes), tok)
        o = dict(eng=eng, fn=_freeze(fn), deps=deps, kind="d", tok=tok, sem=sem, inc=inc)
        self.eng_ops[eng].append(o)
        self.ops.append(o)
        self.dma_since.append(tok)
        return tok

    def wait_all(self, eng, toks):
        o = dict(eng=eng, fn=None, deps=list(toks), kind="w", tok=None)
        self.eng_ops[eng].append(o)
        self.ops.append(o)

    def barrier(self, carry=()):
        carry = dict(carry)
        skip = set(carry.values())
        toks = list(self.last_c.values()) + [t for t in self.dma_since if t not in skip]
        self.dma_since = [t for t in self.dma_since if t in skip]
        for e in ENGS:
            self.wait_all(e, toks)
        self.lastw = {}
        self.readers = {}
        for k, t in carry.items():
            self.lastw[k] = t

    def emit(self, nc):
        tokvc = {}
        cur = {e: {} for e in ENGS}
        milestones = set()

        def covered(clock, t):
            if t[0] == "c":
                return clock.get(("c", t[1]), -1) >= t[2]
            return clock.get(("d", t[1]), 0) >= t[2]

        def merge(clock, other):
            for k, v in other.items():
                if clock.get(k, -1) < v:
                    clock[k] = v

        for o in self.ops:
            e = o["eng"]
            clock = cur[e]
            waits = []
            for t in o["deps"]:
                if t[0] == "c" and t[1] == e and e == "pe":
                    continue
                if covered(clock, t):
                    continue
                waits.append(t)
                merge(clock, tokvc[t])
                if t[0] == "c":
                    milestones.add(t)
            best = {}
            for t in waits:
                k = (t[0], t[1])
                if k not in best or best[k][2] < t[2]:
                    best[k] = t
            o["waits"] = list(best.values())
            if o["tok"] is not None:
                know = dict(clock)
                t = o["tok"]
                k = (t[0], t[1])
                know[k] = max(know.get(k, -1), t[2])
                tokvc[t] = know
        msval = {}
        for e in ENGS:
            c = 0
            for o in self.eng_ops[e]:
                if o["kind"] == "c" and o["tok"] in milestones:
                    c += 1
                    msval[o["tok"]] = c
        esem = {e: nc.alloc_semaphore(name=f"es_{e}") for e in ENGS}
        dsem = {k: nc.alloc_semaphore(name=f"ds_{i}") for i, k in enumerate(self.sem_counts)}
        self.n_sems = len(esem) + len(dsem)

        def run(e, eng):
            for o in self.eng_ops[e]:
                for t in o["waits"]:
                    if t[0] == "c":
                        eng.wait_ge(esem[t[1]], msval[t])
                    else:
                        eng.wait_ge(dsem[t[1]], t[2] * self.sem_inc[t[1]])
                if o["fn"] is None:
                    continue
                ins = o["fn"](eng)
                if o["kind"] == "c":
                    if o["tok"] in milestones:
                        ins.then_inc(esem[e], 1)
                else:
                    ins.then_inc(dsem[o["sem"]], o["inc"])

        with nc.Block() as block:
            @block.tensor
            def _(eng):
                run("pe", eng)

            @block.scalar
            def _(eng):
                run("act", eng)

            @block.vector
            def _(eng):
                run("dve", eng)

            @block.gpsimd
            def _(eng):
                run("pool", eng)

            @block.sync
            def _(eng):
                run("sp", eng)


def build_program(stop_after=None, dbg=False, lite=False):
    nc = bass.Bass("TRN2", target_bir_lowering=False)
    p = Prog()

    def din(name, shape, dt=F32):
        if lite and name[0] == "w":
            shape = [128, 128]
        return nc.dram_tensor(name, list(shape), dt, kind="ExternalInput").ap()

    x_d = din("x", [T, D])
    gains_d = din("gains", [4, 128, D])
    wg_d = [din("wg1", [D, DFF]), din("wg2", [D, DFF])]
    wu_d = [din("wu1", [D, DFF]), din("wu2", [D, DFF])]
    wd_d = [din("wd1", [DFF, D]), din("wd2", [DFF, D])]
    wqkv_d = din("wqkv", [D, 1536])
    wgate_d = din("wgate", [D, 2048])
    bgate_d = din("bgate", [128, 16])
    wa_d = din("wa", [512, D])
    wb_d = din("wb", [512, D])
    wout_d = din("wout", [D, D])
    lamv_d = din("lamv", [128, 4, 64])
    subln_d = din("subln", [128, 1])
    abias_d = din("abias", [128, 2, NJ])
    sel_d = din("sel", [128, 2])
    cb_d = din("cbf", [128, 6, 128], BF16)
    out_d = nc.dram_tensor("out", [T, D], F32, kind="ExternalOutput").ap()
    if dbg:
        dbg_h = nc.dram_tensor("dbg_h", [T, D], F32, kind="ExternalOutput").ap()
        dbg_att = nc.dram_tensor("dbg_att", [512, S], BF16, kind="ExternalOutput").ap()
    ag1_in = [nc.dram_tensor(f"ag1_in{k}", [D, T // 2], BF16) for k in range(2)]
    ag1_out = [nc.dram_tensor(f"ag1_out{k}", [2 * D, T // 2], BF16) for k in range(2)]
    ag2_in = [nc.dram_tensor(f"ag2_in{k}", [256, S], BF16) for k in range(2)]
    ag2_out = [nc.dram_tensor(f"ag2_out{k}", [512, S], BF16) for k in range(2)]

    h = nc.alloc_sbuf_tensor("h", [128, NTB, D], F32)
    cb = nc.alloc_sbuf_tensor("cb", [128, 6, 128], BF16)
    ident, tri, ones, mda, msb, zer = (cb[:, i, :] for i in range(6))
    gbuf = nc.alloc_sbuf_tensor("gbuf", [128, D], F32)
    st = nc.alloc_sbuf_tensor("st", [128, 4, NTB], F32)
    small = nc.alloc_sbuf_tensor("small", [128, 16], F32)
    ARENA = 66560
    arena = nc.alloc_sbuf_tensor("arena", [128, ARENA], BF16)
    aoff = [0]

    def areset():
        aoff[0] = 0

    def take(shape, dt):
        n = 1
        for d_ in shape:
            n *= d_
        ne = n * (2 if dt == F32 else 1)
        ne = (ne + 15) // 16 * 16
        assert aoff[0] + ne <= ARENA, (aoff[0], ne)
        v = arena[:, aoff[0]:aoff[0] + ne]
        aoff[0] += ne
        if dt == F32:
            v = v.bitcast(F32)
        v = v[:, 0:n]
        if len(shape) == 2:
            v = v.rearrange("p (a b) -> p a b", a=shape[0])
        elif len(shape) == 3:
            v = v.rearrange("p (a b c) -> p a b c", a=shape[0], b=shape[1])
        return v

    pp = [nc.alloc_psum_tensor(f"pp{i}", [128, 2, 512], F32) for i in range(4)]
    ps = [pp[i // 2][:, i % 2, :] for i in range(8)]
    psb = [t.bitcast(BF16) for t in ps]

    def PS(i):
        return ("ps", i)

    tx = []
    for q in range(4):
        tx.append(p.dma("sp", lambda e, q=q: e.dma_start(
            out=h[:, q * 4:(q + 1) * 4, :],
            in_=x_d[q * 512:(q + 1) * 512, :].rearrange("(tb p) d -> p tb d", p=128)),
            writes=[("h", q * 4 + i) for i in range(4)], sem=("hload", q)))
    p.dma("sp", lambda e: e.dma_start(out=cb[:], in_=cb_d), writes=["cb"], sem="cb")

    def norm_to_T(gi, dstT, xn, junk, on_tb=None):
        p.dma("sp", lambda e: e.dma_start(out=gbuf[:], in_=gains_d[gi]), writes=["gbuf"], sem="gbuf")
        p.op("dve", lambda e: e.memset(st[:, 0, :], 0.0), writes=["ss"])
        for tb in range(NTB):
            jb = xn[tb % 2]
            p.op("act", lambda e, tb=tb, jb=jb: e.activation(out=jb[:], in_=h[:, tb, :], func=AF.Square,
                                                             accum_out=st[:, 0, tb:tb + 1]),
                 reads=[("h", tb), "ss"], writes=[("ss", tb), ("xn", tb % 2)])
        p.op("dve", lambda e: e.tensor_scalar(out=st[:, 1, :], in0=st[:, 0, :], scalar1=1.0 / D, scalar2=EPS,
                                              op0=ALU.mult, op1=ALU.add), reads=[("ss", i) for i in range(NTB)], writes=["ms"])
        p.op("act", lambda e: e.activation(out=st[:, 2, :], in_=st[:, 1, :], func=AF.Sqrt), reads=["ms"], writes=["sd"])
        p.op("dve", lambda e: e.reciprocal(out=st[:, 3, :], in_=st[:, 2, :]), reads=["sd"], writes=["rstd"])
        import os
        NTBX = int(os.environ.get("NTBX", NTB))
        for tb in range(NTBX):
            sl = tb % 2
            if os.environ.get("NOSTT"):
                p.op("dve", lambda e, tb=tb, sl=sl: e.tensor_copy(out=xn[sl][:], in_=h[:, tb, :]),
                     reads=[("h", tb), "rstd", "gbuf"], writes=[("xn", sl)])
            else:
                p.op("dve", lambda e, tb=tb, sl=sl: e.scalar_tensor_tensor(
                    out=xn[sl][:], in0=h[:, tb, :], scalar=st[:, 3, tb:tb + 1], in1=gbuf[:],
                    op0=ALU.mult, op1=ALU.mult), reads=[("h", tb), "rstd", "gbuf"], writes=[("xn", sl)])
            bA, bB = (6, 7) if tb % 2 == 0 else (4, 5)
            for c in range(8):
                bank = bA if c < 4 else bB
                p.op("pe", lambda e, c=c, sl=sl, bank=bank: e.transpose(
                    out=psb[bank][:, (c % 4) * 128:(c % 4 + 1) * 128], in_=xn[sl][:, c * 128:(c + 1) * 128], identity=ident),
                    reads=[("xn", sl), "cb"], writes=[PS(bank)])
            srcA = psb[bA][:, 0:512].rearrange("p (c t) -> p c t", c=4)
            srcB = psb[bB][:, 0:512].rearrange("p (c t) -> p c t", c=4)
            p.op("act", lambda e, tb=tb, srcA=srcA: e.copy(out=dstT[:, 0:4, tb * 128:(tb + 1) * 128], in_=srcA),
                 reads=[PS(bA)], writes=[("xT", tb // 4, 0)])
            p.op("dve", lambda e, tb=tb, srcB=srcB: e.tensor_copy(out=dstT[:, 4:8, tb * 128:(tb + 1) * 128], in_=srcB),
                 reads=[PS(bB)], writes=[("xT", tb // 4, 1)])
            if on_tb is not None:
                on_tb(tb)

    def xT_keys(tt):
        return [("xT", tt, 0), ("xT", tt, 1)]

    def ffn(fi, on_final=None, pre_final=None):
        areset()
        xnT = take([8, T], BF16)
        actT = take([6, T], BF16)
        wdb = [take([6, D], BF16) for i in range(2)]
        wgb = [take([8, 256], BF16) for i in range(2)]
        wub = [take([8, 256], BF16) for i in range(2)]
        xn = [take([D], BF16) for i in range(2)]
        sg = [take([512], F32) for i in range(2)]
        junk = take([D], BF16)
        norm_to_T(0 if fi == 0 else 2, xnT, xn, junk)
        if stop_after == "norm1":
            p.barrier()
            return
        wgv = wg_d[fi].rearrange("(c p) f -> p c f", p=128)
        wuv = wu_d[fi].rearrange("(c p) f -> p c f", p=128)
        wdv = wd_d[fi].rearrange("(c p) m -> p c m", p=128)
        gi = 0
        it = 0
        dn = 0
        for pi, (fc0, nfc) in enumerate(PARTS):
            if stop_after == "part1" and pi >= 1:
                break
            wsl = pi % 2
            p.dma("pool", lambda e, wsl=wsl, fc0=fc0, nfc=nfc: e.dma_start(
                out=wdb[wsl][:, 0:nfc, :], in_=wdv[:, fc0:fc0 + nfc, :]), writes=[("wd", wsl)], sem=("wd", fi, wsl))
            groups = []
            k = 0
            while k < nfc:
                n = min(2, nfc - k)
                groups.append((k, n))
                k += n
            for (k0, n) in groups:
                gs = gi % 2
                gi += 1
                col0 = (fc0 + k0) * 128
                p.dma("pool", lambda e, gs=gs, col0=col0, n=n: e.dma_start(
                    out=wgb[gs][:, :, 0:n * 128], in_=wgv[:, :, col0:col0 + n * 128]), writes=[("wg", gs)], sem=("wg", fi, gs))
                p.dma("pool", lambda e, gs=gs, col0=col0, n=n: e.dma_start(
                    out=wub[gs][:, :, 0:n * 128], in_=wuv[:, :, col0:col0 + n * 128]), writes=[("wu", gs)], sem=("wu", fi, gs))
                for jj in range(n):
                    fcl = k0 + jj
                    for tt in range(NTT):
                        b = it % 2
                        it += 1
                        for c in range(8):
                            p.op("pe", lambda e, gs=gs, jj=jj, c=c, tt=tt, b=b: e.matmul(
                                ps[b][:], lhsT=wgb[gs][:, c, jj * 128:(jj + 1) * 128], rhs=xnT[:, c, tt * 512:(tt + 1) * 512],
                                start=(c == 0), stop=(c == 7)), reads=[("wg", gs)] + xT_keys(tt), writes=[PS(b)])
                        for c in range(8):
                            p.op("pe", lambda e, gs=gs, jj=jj, c=c, tt=tt, b=b: e.matmul(
                                ps[2 + b][:], lhsT=wub[gs][:, c, jj * 128:(jj + 1) * 128], rhs=xnT[:, c, tt * 512:(tt + 1) * 512],
                                start=(c == 0), stop=(c == 7)), reads=[("wu", gs)] + xT_keys(tt), writes=[PS(2 + b)])
                        p.op("act", lambda e, b=b: e.activation(out=sg[b][:], in_=ps[b][:], func=AF.Silu),
                             reads=[PS(b)], writes=[("sg", b)])
                        p.op("dve", lambda e, b=b, fcl=fcl, tt=tt: e.tensor_tensor(
                            out=actT[:, fcl, tt * 512:(tt + 1) * 512], in0=sg[b][:], in1=ps[2 + b][:], op=ALU.mult),
                            reads=[("sg", b), PS(2 + b)], writes=[("actT", fcl, tt)])
            lastp = (pi == len(PARTS) - 1)
            if lastp and pre_final is not None:
                pre_final(xn)
            for tb in range(NTB):
                for mh in range(2):
                    b = 4 + dn % 2
                    dn += 1
                    for fcl in range(nfc):
                        p.op("pe", lambda e, fcl=fcl, tb=tb, mh=mh, b=b, wsl=wsl, nfc=nfc: e.matmul(
                            ps[b][:], lhsT=actT[:, fcl, tb * 128:(tb + 1) * 128], rhs=wdb[wsl][:, fcl, mh * 512:(mh + 1) * 512],
                            start=(fcl == 0), stop=(fcl == nfc - 1)),
                            reads=[("actT", fcl, tb // 4), ("wd", wsl)], writes=[PS(b)])
                    p.op("dve", lambda e, tb=tb, mh=mh, b=b: e.scalar_tensor_tensor(
                        out=h[:, tb, mh * 512:(mh + 1) * 512], in0=ps[b][:], scalar=0.5, in1=h[:, tb, mh * 512:(mh + 1) * 512],
                        op0=ALU.mult, op1=ALU.add), reads=[PS(b), ("h", tb)], writes=[("h", tb)])
                if lastp and on_final is not None:
                    on_final(tb)
        if fi == 0 or stop_after == "ffn2":
            p.barrier()
        return xn

    def dump_h(tag):
        toks = []
        for q in range(4):
            toks.append(p.dma("sp", lambda e, q=q: e.dma_start(
                out=dbg_h[q * 512:(q + 1) * 512, :].rearrange("(tb p) d -> p tb d", p=128), in_=h[:, q * 4:(q + 1) * 4, :]),
                reads=[("h", q * 4 + i) for i in range(4)], sem=("dbgh", tag, q)))
        return toks

    def finish(extra):
        p.wait_all("sp", extra)
        p.emit(nc)
        return nc

    if stop_after == "load":
        return finish(dump_h("l"))
    if not os.environ.get("SKIPFFN"):
        ffn(0)
    if stop_after in ("ffn1", "norm1", "part1"):
        return finish(dump_h("a"))

    areset()
    wqk = take([8, 768], BF16)
    wqv = wqkv_d.rearrange("(c p) f -> p c f", p=128)
    tok_wqk = p.dma("pool", lambda e: e.dma_start(out=wqk, in_=wqv[:, :, 0:768]), writes=["wqk"], sem="wqk")
    nT = take([8, T], BF16)
    xn0 = [take([D], BF16) for i in range(2)]
    junk0 = take([D], BF16)
    cc1_tok = {}

    def after_tb(tb):
        if tb % 8 != 7:
            return
        k = tb // 8
        for c in range(8):
            p.dma("sp", lambda e, c=c, k=k: e.dma_start(out=ag1_in[k].ap()[c * 128:(c + 1) * 128, :],
                                                       in_=nT[:, c, k * 1024:(k + 1) * 1024]),
                  reads=xT_keys(2 * k) + xT_keys(2 * k + 1), writes=[("ag1_in", k, c)], sem=("ag1w", k, c))
        cc1_tok[k] = p.dma("pool", lambda e, k=k: e.collective_compute("AllGather", ALU.bypass, replica_groups=PAIRS,
                                                                     ins=[ag1_in[k].ap().opt()], outs=[ag1_out[k].ap().opt()]),
                           reads=[("ag1_in", k, c) for c in range(8)], writes=[("ag1_out", k)], sem=("cc1", k), inc=1)

    norm_to_T(1, nT, xn0, junk0, on_tb=after_tb)
    lamv = nc.alloc_sbuf_tensor("lamv_sb", [128, 4, 64], F32)
    lprod = nc.alloc_sbuf_tensor("lprod_sb", [128, 2, 64], F32)
    subl = nc.alloc_sbuf_tensor("subl_sb", [128, 2], F32)
    abias = nc.alloc_sbuf_tensor("abias_sb", [128, 2, NJ], F32)
    sel = nc.alloc_sbuf_tensor("sel_sb", [128, 2], F32)
    bgate = nc.alloc_sbuf_tensor("bgate_sb", [128, 16], F32)
    p.dma("sp", lambda e: e.dma_start(out=lamv[:], in_=lamv_d), writes=["lamv"], sem="lamv")
    p.dma("sp", lambda e: e.dma_start(out=subl[:, 0:1], in_=subln_d), writes=["subl0"], sem="subl")
    p.dma("sp", lambda e: e.dma_start(out=abias[:], in_=abias_d), writes=["abias"], sem="abias")
    p.dma("sp", lambda e: e.dma_start(out=sel[:], in_=sel_d), writes=["sel"], sem="sel")
    p.dma("sp", lambda e: e.dma_start(out=bgate[:], in_=bgate_d), writes=["bgate"], sem="bgate")
    p.op("dve", lambda e: e.tensor_tensor(out=lprod[:, 0, :], in0=lamv[:, 0, :], in1=lamv[:, 1, :], op=ALU.mult),
         reads=["lamv"], writes=["lprod0"])
    p.op("dve", lambda e: e.tensor_tensor(out=lprod[:, 1, :], in0=lamv[:, 2, :], in1=lamv[:, 3, :], op=ALU.mult),
         reads=["lamv"], writes=["lprod1"])
    p.op("dve", lambda e: e.reduce_sum(out=small[:, 0:2], in_=lprod[:], axis=mybir.AxisListType.X),
         reads=["lprod0", "lprod1"], writes=["sm01"])
    p.op("act", lambda e: e.activation(out=small[:, 2:4], in_=small[:, 0:2], func=AF.Exp), reads=["sm01"], writes=["sm23"])
    p.op("dve", lambda e: e.tensor_tensor(out=small[:, 4:5], in0=small[:, 3:4], in1=small[:, 2:3], op=ALU.subtract),
         reads=["sm23"], writes=["sm4"])
    p.op("dve", lambda e: e.tensor_scalar(out=small[:, 5:6], in0=small[:, 4:5], scalar1=-LAM_INIT, scalar2=0.0,
                                          op0=ALU.add, op1=ALU.add), reads=["sm4"], writes=["neglam"])
    p.op("dve", lambda e: e.tensor_scalar(out=subl[:, 1:2], in0=subl[:, 0:1], scalar1=1.0 - LAM_INIT, scalar2=0.0,
                                          op0=ALU.mult, op1=ALU.add), reads=["subl0"], writes=["subl1"])
    neglam = small[:, 5:6]
    carry = {("ag1_out", k): cc1_tok[k] for k in range(2)}
    carry["wqk"] = tok_wqk
    p.barrier(carry=carry)
    if stop_after == "ag1":
        return finish(dump_h("b"))

    areset()
    wqk = take([8, 768], BF16)
    nts = [take([8, 512], BF16) for i in range(2)]
    QT = take([2, S], BF16)
    KT = take([2, S], BF16)
    V = take([32, 256], BF16)
    Pp = [take([2, 512], BF16) for i in range(2)]
    Pm = [[Pp[i][:, m, :] for i in range(2)] for m in range(2)]
    fin = [take([512], F32) for i in range(4)]
    fin2 = [take([512], F32) for i in range(4)]
    sqb = take([512], BF16)
    sqb2 = take([512], BF16)
    acc0 = [take([512], F32) for i in range(2)]
    accb = [take([512], BF16) for i in range(2)]
    aout = [take([512], BF16) for i in range(2)]
    Eb = [take([512], F32) for i in range(4)]
    Ub = [take([512], BF16) for i in range(2)]
    Xb = [take([512], F32) for i in range(2)]
    Ab = [take([512], BF16) for i in range(4)]
    Usum2 = [take([512], BF16) for i in range(2)]
    ag1v = [ag1_out[k].ap().rearrange("(r c p) t -> p r c t", r=2, c=8, p=128) for k in range(2)]
    aoi = [0]

    def project(br):
        qscale = 1.0 if br == 0 else 128.0 ** -0.5
        for ti, t8 in enumerate([0, 1, 4, 5, 2, 3, 6, 7]):
            r, tl = t8 // 4, t8 % 4
            sl = ti % 2
            piece, tl2 = tl // 2, tl % 2
            p.dma("sp", lambda e, r=r, tl2=tl2, sl=sl, piece=piece: e.dma_start(
                out=nts[sl][:], in_=ag1v[piece][:, r, :, tl2 * 512:(tl2 + 1) * 512]),
                reads=[("ag1_out", piece)], writes=[("nts", sl, 0), ("nts", sl, 1)], sem=("nts", sl))
            cnt = 0
            for which in range(2):
                for hl in range(2):
                    b = cnt % 2
                    cnt += 1
                    col = which * 256 + hl * 128
                    for c in range(8):
                        p.op("pe", lambda e, c=c, sl=sl, col=col, b=b: e.matmul(
                            ps[b][:], lhsT=wqk[:, c, col:col + 128], rhs=nts[sl][:, c, :], start=(c == 0), stop=(c == 7)),
                            reads=["wqk", ("nts", sl, 0), ("nts", sl, 1)], writes=[PS(b)])
                    if which == 0:
                        p.op("act", lambda e, hl=hl, t8=t8, b=b: e.mul(out=QT[:, hl, t8 * 512:(t8 + 1) * 512], in_=ps[b][:], mul=qscale),
                             reads=[PS(b)], writes=[("QT", hl, t8)])
                    else:
                        p.op("dve", lambda e, hl=hl, t8=t8, b=b: e.tensor_copy(out=KT[:, hl, t8 * 512:(t8 + 1) * 512], in_=ps[b][:]),
                             reads=[PS(b)], writes=[("KT", hl, t8)])
            for tb4 in range(4):
                b = 2 + tb4 % 2
                for c in range(8):
                    p.op("pe", lambda e, c=c, sl=sl, tb4=tb4, b=b: e.matmul(
                        ps[b][:, 0:256], lhsT=nts[sl][:, c, tb4 * 128:(tb4 + 1) * 128], rhs=wqk[:, c, 512:768],
                        start=(c == 0), stop=(c == 7)), reads=["wqk", ("nts", sl, 0), ("nts", sl, 1)], writes=[PS(b)])
                if tb4 % 2 == 0:
                    p.op("act", lambda e, t8=t8, tb4=tb4, b=b: e.copy(out=V[:, t8 * 4 + tb4, :], in_=ps[b][:, 0:256]),
                         reads=[PS(b)], writes=[("V", t8)])
                else:
                    p.op("dve", lambda e, t8=t8, tb4=tb4, b=b: e.tensor_copy(out=V[:, t8 * 4 + tb4, :], in_=ps[b][:, 0:256]),
                         reads=[PS(b)], writes=[("V", t8)])

    def store_att(br, hl, qt, src_sl):
        row0 = hl * 128
        p.dma("sp", lambda e: e.dma_start(out=ag2_in[br].ap()[row0:row0 + 128, qt * 512:(qt + 1) * 512], in_=aout[src_sl][:]),
              reads=[("aout", src_sl)], writes=[("ag2_in", br, hl, qt)], sem=("aout", src_sl))

    def run_pipeline(nblocks, stages, skews, deferred):
        last = nblocks + max(skews)
        it = 0
        while it < last or any(k >= it for k in deferred):
            for st_fn, sk in zip(stages, skews):
                g = it - sk
                if 0 <= g < nblocks:
                    st_fn(g)
            for fn in deferred.pop(it, []):
                fn()
            it += 1
            if it > last + 64:
                break
        for k in sorted(deferred):
            for fn in deferred[k]:
                fn()
        deferred.clear()

    def attn_da():
        blocks = [(hl, qt, kb) for hl in range(2) for qt in range(8) for kb in range(4 * qt + 4)]
        NB = len(blocks)
        deferred = {}
        fsets = [fin, fin2]
        sqbs = [sqb, sqb2]

        def geom(g):
            hl, qt, kb = blocks[g]
            j = kb - 4 * qt
            return hl, qt, kb, j, (128 * j if j > 0 else 0)

        def stage_scores(g):
            hl, qt, kb, j, c0 = geom(g)
            par = g % 2
            for m in range(2):
                bank = par * 2 + m
                p.op("pe", lambda e, m=m, kb=kb, c0=c0, bank=bank, hl=hl, qt=qt: e.matmul(
                    ps[bank][:, c0:512], lhsT=KT[64 * m:64 * m + 64, hl, kb * 128:(kb + 1) * 128],
                    rhs=QT[64 * m:64 * m + 64, hl, qt * 512 + c0:(qt + 1) * 512], start=True, stop=True),
                    reads=[("KT", hl, kb // 4), ("QT", hl, qt)], writes=[PS(bank)])
            p.op("act", lambda e, c0=c0, par=par, j=j, hl=hl: e.activation(
                out=Pp[par][:, :, c0:512], in_=pp[par][:, :, c0:512], func=AF.Exp,
                bias=abias[:, hl, j + 28:j + 29], scale=0.125),
                reads=[PS(par * 2), PS(par * 2 + 1), "abias"], writes=[("P", 0, par), ("P", 1, par)])
            if j >= 0:
                for m in range(2):
                    p.op("dve", lambda e, m=m, c0=128 * j, par=par: e.tensor_tensor(
                        out=Pm[m][par][:, c0:c0 + 128], in0=Pm[m][par][:, c0:c0 + 128], in1=mda, op=ALU.mult),
                        reads=[("P", m, par), "cb"], writes=[("P", m, par)])

        def stage_acc(g):
            hl, qt, kb, j, c0 = geom(g)
            par = g % 2
            nkb = 4 * qt + 4
            for m in range(2):
                p.op("pe", lambda e, m=m, kb=kb, c0=c0, par=par, hl=hl, nkb=nkb: e.matmul(
                    ps[4 + m][:, c0:512], lhsT=V[:, kb, hl * 128:(hl + 1) * 128], rhs=Pm[m][par][:, c0:512],
                    start=(kb == 0), stop=(kb == nkb - 1)),
                    reads=[("V", kb // 4), ("P", m, par)], writes=[PS(4 + m)])
                if m == 1:
                    p.op("pe", lambda e, m=m, c0=c0, par=par, kb=kb, nkb=nkb: e.matmul(
                        ps[6 + m][:, c0:512], lhsT=ones, rhs=Pm[m][par][:, c0:512],
                        start=(kb == 0), stop=(kb == nkb - 1)),
                        reads=["cb", ("P", m, par)], writes=[PS(6 + m)])
                else:
                    tpa = (hl * 8 + qt) % 2
                    if kb == 0:
                        p.op("dve", lambda e, par=par, tpa=tpa: e.tensor_copy(out=acc0[tpa][:], in_=Pm[0][par][:]),
                             reads=[("P", 0, par)], writes=[("acc0", tpa)])
                    else:
                        p.op("dve", lambda e, par=par, tpa=tpa, c0=c0: e.tensor_tensor(
                            out=acc0[tpa][:, c0:512], in0=acc0[tpa][:, c0:512], in1=Pm[0][par][:, c0:512], op=ALU.add),
                            reads=[("P", 0, par), ("acc0", tpa)], writes=[("acc0", tpa)])
            if kb == nkb - 1:
                tp = (hl * 8 + qt) % 2
                p.op("dve", lambda e, tp=tp: e.tensor_copy(out=accb[tp][:], in_=acc0[tp][:]), reads=[("acc0", tp)], writes=[("accb", tp)])
                p.op("pe", lambda e, tp=tp: e.matmul(ps[6][:], lhsT=ones, rhs=accb[tp][:], start=True, stop=True),
                     reads=["cb", ("accb", tp)], writes=[PS(6)])
                l0s, l1s, o0s, o1s = fsets[tp]
                sq = sqbs[tp]
                K = lambda n: (n, tp)
                p.op("act", lambda e, l0s=l0s: e.copy(out=l0s[:], in_=ps[6][:]), reads=[PS(6)], writes=[K("l0s")])
                p.op("dve", lambda e, o0s=o0s: e.tensor_copy(out=o0s[:], in_=ps[4][:]), reads=[PS(4)], writes=[K("o0s")])
                p.op("act", lambda e, l1s=l1s: e.copy(out=l1s[:], in_=ps[7][:]), reads=[PS(7)], writes=[K("l1s")])
                p.op("dve", lambda e, o1s=o1s: e.tensor_copy(out=o1s[:], in_=ps[5][:]), reads=[PS(5)], writes=[K("o1s")])
                p.op("dve", lambda e, l0s=l0s: e.reciprocal(out=l0s[:], in_=l0s[:]), reads=[K("l0s")], writes=[K("l0s")])
                p.op("dve", lambda e, l1s=l1s: e.reciprocal(out=l1s[:], in_=l1s[:]), reads=[K("l1s")], writes=[K("l1s")])
                p.op("dve", lambda e, o0s=o0s, l0s=l0s: e.tensor_tensor(out=o0s[:], in0=o0s[:], in1=l0s[:], op=ALU.mult),
                     reads=[K("o0s"), K("l0s")], writes=[K("o0s")])
                p.op("dve", lambda e, o1s=o1s, l1s=l1s: e.tensor_tensor(out=o1s[:], in0=o1s[:], in1=l1s[:], op=ALU.mult),
                     reads=[K("o1s"), K("l1s")], writes=[K("o1s")])
                p.op("dve", lambda e, o0s=o0s, o1s=o1s: e.scalar_tensor_tensor(out=o0s[:], in0=o1s[:], scalar=neglam, in1=o0s[:],
                                                                           op0=ALU.mult, op1=ALU.add),
                     reads=[K("o1s"), K("o0s"), "neglam"], writes=[K("o0s")])
                it_now = g + 1

                def fin_b(o0s=o0s, l0s=l0s, sq=sq, K=K):
                    p.op("act", lambda e: e.activation(out=sq[:], in_=o0s[:], func=AF.Square), reads=[K("o0s")], writes=[K("sq")])
                    p.op("pe", lambda e: e.matmul(ps[0][:], lhsT=ones, rhs=sq[:], start=True, stop=True),
                         reads=["cb", K("sq")], writes=[PS(0)])
                    p.op("dve", lambda e: e.tensor_scalar(out=l0s[:], in0=ps[0][:], scalar1=1.0 / 128, scalar2=EPS,
                                                          op0=ALU.mult, op1=ALU.add), reads=[PS(0)], writes=[K("l0s")])

                def fin_c(hl=hl, qt=qt, o0s=o0s, l0s=l0s, l1s=l1s, K=K):
                    p.op("act", lambda e: e.activation(out=l1s[:], in_=l0s[:], func=AF.Ln), reads=[K("l0s")], writes=[K("l1s")])
                    p.op("act", lambda e: e.activation(out=l0s[:], in_=l1s[:], func=AF.Exp, scale=-0.5), reads=[K("l1s")], writes=[K("l0s")])
                    sl = aoi[0] % 2
                    aoi[0] += 1
                    p.op("dve", lambda e, sl=sl: e.scalar_tensor_tensor(out=aout[sl][:], in0=o0s[:], scalar=subl[:, 1:2], in1=l0s[:],
                                                                        op0=ALU.mult, op1=ALU.mult),
                         reads=[K("o0s"), K("l0s"), "subl1"], writes=[("aout", sl)])
                    store_att(0, hl, qt, sl)

                deferred.setdefault(it_now + 3, []).append(fin_b)
                deferred.setdefault(it_now + 5, []).append(fin_c)

        run_pipeline(NB, [stage_scores, stage_acc], [0, 1], deferred)

    def attn_sb():
        blocks = []
        for hl in range(2):
            for qt in range(8):
                nkb = 4 * qt + 4
                for i, kb in enumerate(range(nkb - 1, -1, -1)):
                    blocks.append((hl, qt, kb, i, nkb))
        NB = len(blocks)
        deferred = {}

        def geom(g):
            hl, qt, kb, i, nkb = blocks[g]
            j = kb - 4 * qt
            tile = hl * 8 + qt
            return hl, qt, kb, i, nkb, j, (128 * j if j > 0 else 0), tile

        def st1(g):
            hl, qt, kb, i, nkb, j, c0, tile = geom(g)
            par = g % 2
            e3 = g % 4
            p.op("pe", lambda e, kb=kb, c0=c0, par=par, hl=hl, qt=qt: e.matmul(
                ps[par][:, c0:512], lhsT=KT[:, hl, kb * 128:(kb + 1) * 128],
                rhs=QT[:, hl, qt * 512 + c0:(qt + 1) * 512], start=True, stop=True),
                reads=[("KT", hl, kb // 4), ("QT", hl, qt)], writes=[PS(par)])
            p.op("act", lambda e, c0=c0, par=par, e3=e3: e.activation(out=Eb[e3][:, c0:512], in_=ps[par][:, c0:512], func=AF.Exp),
                 reads=[PS(par)], writes=[("E", e3)])

        def st1b(g):
            hl, qt, kb, i, nkb, j, c0, tile = geom(g)
            par = g % 2
            e3 = g % 4
            p.op("act", lambda e, c0=c0, par=par, e3=e3: e.activation(out=Ub[par][:, c0:512], in_=Eb[e3][:, c0:512], func=AF.Ln, bias=1.0),
                 reads=[("E", e3)], writes=[("U", par)])
            if j >= 0:
                p.op("dve", lambda e, c0=128 * j, par=par: e.tensor_tensor(
                    out=Ub[par][:, c0:c0 + 128], in0=Ub[par][:, c0:c0 + 128], in1=msb, op=ALU.mult),
                    reads=[("U", par), "cb"], writes=[("U", par)])

        def st2(g):
            hl, qt, kb, i, nkb, j, c0, tile = geom(g)
            par = g % 2
            e3 = g % 4
            us_r = Usum2[i % 2]
            us_w = Usum2[(i + 1) % 2]
            kr = ("usum", i % 2)
            kw = ("usum", (i + 1) % 2)
            first = (i == 0)
            p.op("pe", lambda e, c0=c0, par=par, first=first: e.matmul(
                ps[2 + par][:, c0:512], lhsT=tri, rhs=Ub[par][:, c0:512], start=True, stop=first),
                reads=["cb", ("U", par)], writes=[PS(2 + par)])
            if not first:
                p.op("pe", lambda e, c0=c0, par=par, us_r=us_r: e.matmul(
                    ps[2 + par][:, c0:512], lhsT=ones, rhs=us_r[:, c0:512], start=False, stop=True),
                    reads=["cb", kr], writes=[PS(2 + par)])
            if i < nkb - 1:
                if first:
                    p.op("pool", lambda e, us_w=us_w: e.memset(us_w[:], 0.0), writes=[kw])
                    p.op("pool", lambda e, c0=c0, par=par, us_w=us_w: e.tensor_copy(out=us_w[:, c0:512], in_=Ub[par][:, c0:512]),
                         reads=[("U", par)], writes=[kw])
                else:
                    if c0 > 0:
                        p.op("pool", lambda e, c0=c0, us_w=us_w, us_r=us_r: e.tensor_copy(out=us_w[:, 0:c0], in_=us_r[:, 0:c0]),
                             reads=[kr], writes=[kw])
                    p.op("pool", lambda e, c0=c0, par=par, us_w=us_w, us_r=us_r: e.tensor_tensor(
                        out=us_w[:, c0:512], in0=us_r[:, c0:512], in1=Ub[par][:, c0:512], op=ALU.add),
                        reads=[kr, ("U", par)], writes=[kw])

        def st2b(g):
            hl, qt, kb, i, nkb, j, c0, tile = geom(g)
            par = g % 2
            e3 = g % 4
            p.op("act", lambda e, c0=c0, par=par: e.activation(out=Xb[par][:, c0:512], in_=ps[2 + par][:, c0:512],
                                                             func=AF.Exp, scale=-1.0),
                 reads=[PS(2 + par)], writes=[("X", par)])
            p.op("dve", lambda e, c0=c0, par=par, e3=e3: e.tensor_tensor(
                out=Ab[e3][:, c0:512], in0=Eb[e3][:, c0:512], in1=Xb[par][:, c0:512], op=ALU.mult),
                reads=[("E", e3), ("X", par)], writes=[("A", e3)])
            if j >= 0:
                p.op("dve", lambda e, c0=128 * j, e3=e3: e.tensor_tensor(
                    out=Ab[e3][:, c0:c0 + 128], in0=Ab[e3][:, c0:c0 + 128], in1=msb, op=ALU.mult),
                    reads=[("A", e3), "cb"], writes=[("A", e3)])

        def st3(g):
            hl, qt, kb, i, nkb, j, c0, tile = geom(g)
            e3 = g % 4
            ob = 4 + tile % 2
            if i == 0:
                p.op("pe", lambda e, ob=ob, hl=hl, qt=qt: e.matmul(ps[ob][:], lhsT=zer, rhs=QT[:, hl, qt * 512:(qt + 1) * 512],
                                                                 start=True, stop=False),
                     reads=["cb", ("QT", hl, qt)], writes=[PS(ob)])
            p.op("pe", lambda e, kb=kb, c0=c0, e3=e3, ob=ob, hl=hl, last=(i == nkb - 1): e.matmul(
                ps[ob][:, c0:512], lhsT=V[:, kb, hl * 128:(hl + 1) * 128], rhs=Ab[e3][:, c0:512],
                start=False, stop=last),
                reads=[("V", kb // 4), ("A", e3)], writes=[PS(ob)])
            if i == nkb - 1:
                def fin(hl=hl, qt=qt, ob=ob):
                    sl = aoi[0] % 2
                    aoi[0] += 1
                    p.op("dve", lambda e, sl=sl, ob=ob: e.tensor_copy(out=aout[sl][:], in_=ps[ob][:]), reads=[PS(ob)], writes=[("aout", sl)])
                    store_att(1, hl, qt, sl)
                deferred.setdefault(g + 3 + 2, []).append(fin)

        run_pipeline(NB, [st1, st2, st2b, st1b, st3], [0, 1, 2, 0, 3], deferred)

    def dump_att():
        toks = []
        for k in range(2):
            toks.append(p.dma("sp", lambda e, k=k: e.dma_start(out=dbg_att[k * 256:(k + 1) * 256, :], in_=ag2_in[k].ap()),
                              sem=("dbgatt", k)))
        return toks

    def ag2(k):
        p.dma("pool", lambda e, k=k: e.collective_compute("AllGather", ALU.bypass, replica_groups=PAIRS,
                                                          ins=[ag2_in[k].ap().opt()], outs=[ag2_out[k].ap().opt()]),
              writes=[("ag2_out", k)], sem=("cc2", k), inc=1)

    project(0)
    p.dma("pool", lambda e: e.dma_start(out=wqk[:], in_=wqv[:, :, 768:1536]), writes=["wqk"], sem="wqk")
    attn_da()
    p.barrier()
    if stop_after == "da":
        return finish(dump_att())
    ag2(0)
    project(1)
    attn_sb()
    p.barrier()
    if stop_after == "sb":
        return finish(dump_att())
    ag2(1)
    p.barrier()

    areset()
    wgate = take([8, 2048], BF16)
    wout = take([8, D], BF16)
    wab = take([2, 4, D], BF16)
    nts = [take([8, 512], BF16) for i in range(2)]
    ah = [take([8, 512], BF16) for i in range(2)]
    am = take([8, 512], BF16)
    gts = [take([512], F32) for i in range(4)]
    yT = take([8, 512], BF16)
    p.dma("pool", lambda e: e.dma_start(out=wgate, in_=wgate_d.rearrange("(c p) m -> p c m", p=128)), writes=["wgate"], sem="wgate")
    p.dma("pool", lambda e: e.dma_start(out=wab[:, 0, :, :], in_=wa_d.rearrange("(c p) m -> p c m", p=128)), writes=["wa"], sem="wa")
    p.dma("pool", lambda e: e.dma_start(out=wab[:, 1, :, :], in_=wb_d.rearrange("(c p) m -> p c m", p=128)), writes=["wb"], sem="wb")
    p.dma("pool", lambda e: e.dma_start(out=wout, in_=wout_d.rearrange("(c p) m -> p c m", p=128)), writes=["wout"], sem="wout")
    ag1own = [ag1_in[k].ap().rearrange("(c p) t -> p c t", p=128) for k in range(2)]
    ag2v = [ag2_out[k].ap().rearrange("(q p) t -> p q t", p=128) for k in range(2)]
    for tt in range(NTT):
        sl = tt % 2
        p.dma("sp", lambda e, tt=tt, sl=sl: e.dma_start(
            out=nts[sl][:], in_=ag1own[tt // 2][:, :, (tt % 2) * 512:(tt % 2 + 1) * 512]),
            writes=[("nts", sl, 0), ("nts", sl, 1)], sem=("nts", sl))
        for half in range(2):
            for k in range(2):
                p.dma("sp", lambda e, tt=tt, half=half, k=k: e.dma_start(
                    out=ah[half][:, k * 4:(k + 1) * 4, :], in_=ag2v[k][:, :, half * T + tt * 512: half * T + (tt + 1) * 512]),
                    writes=[("ah", half, k)], sem=("ah", half, k))
        p.op("dve", lambda e: e.tensor_scalar(out=am[:], in0=ah[0][:], scalar1=sel[:, 0:1], scalar2=None, op0=ALU.mult),
             reads=[("ah", 0, 0), ("ah", 0, 1), "sel"], writes=["am"])
        p.op("dve", lambda e: e.scalar_tensor_tensor(out=am[:], in0=ah[1][:], scalar=sel[:, 1:2], in1=am[:],
                                                     op0=ALU.mult, op1=ALU.add),
             reads=[("ah", 1, 0), ("ah", 1, 1), "sel", "am"], writes=["am"])
        for mc in range(8):
            par = mc % 2
            for br in range(2):
                bank = br * 2 + par
                col = br * 1024 + mc * 128
                for c in range(8):
                    p.op("pe", lambda e, c=c, col=col, bank=bank, sl=sl: e.matmul(
                        ps[bank][:], lhsT=wgate[:, c, col:col + 128], rhs=nts[sl][:, c, :], start=(c == 0), stop=(c == 7)),
                        reads=["wgate", ("nts", sl, 0), ("nts", sl, 1)], writes=[PS(bank)])
                p.op("act", lambda e, br=br, bank=bank, par=par, mc=mc: e.activation(
                    out=gts[br * 2 + par][:], in_=ps[bank][:], func=AF.Sigmoid, bias=bgate[:, br * 8 + mc: br * 8 + mc + 1]),
                    reads=[PS(bank), "bgate"], writes=[("gt", br, par)])
            for br in range(2):
                bank = 4 + br * 2 + par
                for hh in range(4):
                    k = br * 4 + hh
                    p.op("pe", lambda e, hh=hh, k=k, br=br, bank=bank, mc=mc: e.matmul(
                        ps[bank][:], lhsT=wab[:, br, hh, mc * 128:(mc + 1) * 128], rhs=am[:, k, :], start=(hh == 0), stop=(hh == 3)),
                        reads=["wa" if br == 0 else "wb", "am"], writes=[PS(bank)])
            p.op("dve", lambda e, par=par: e.tensor_tensor(out=gts[par][:], in0=gts[par][:], in1=ps[4 + par][:], op=ALU.mult),
                 reads=[("gt", 0, par), PS(4 + par)], writes=[("gt", 0, par)])
            p.op("dve", lambda e, par=par: e.tensor_tensor(out=gts[2 + par][:], in0=gts[2 + par][:], in1=ps[6 + par][:], op=ALU.mult),
                 reads=[("gt", 1, par), PS(6 + par)], writes=[("gt", 1, par)])
            p.op("dve", lambda e, par=par, mc=mc: e.tensor_tensor(out=yT[:, mc, :], in0=gts[par][:], in1=gts[2 + par][:], op=ALU.add),
                 reads=[("gt", 0, par), ("gt", 1, par)], writes=[("yT", mc)])
        for tb4 in range(4):
            tb = tt * 4 + tb4
            for mh in range(2):
                bank = (tb4 * 2 + mh) % 2
                for mc in range(8):
                    p.op("pe", lambda e, mc=mc, tb4=tb4, mh=mh, bank=bank: e.matmul(
                        ps[bank][:], lhsT=yT[:, mc, tb4 * 128:(tb4 + 1) * 128], rhs=wout[:, mc, mh * 512:(mh + 1) * 512],
                        start=(mc == 0), stop=(mc == 7)), reads=[("yT", mc), "wout"], writes=[PS(bank)])
                p.op("dve", lambda e, tb=tb, mh=mh, bank=bank: e.tensor_tensor(
                    out=h[:, tb, mh * 512:(mh + 1) * 512], in0=ps[bank][:], in1=h[:, tb, mh * 512:(mh + 1) * 512], op=ALU.add),
                    reads=[PS(bank), ("h", tb)], writes=[("h", tb)])
    p.barrier()
    if stop_after == "merge":
        return finish(dump_h("d"))

    fstate = {}
    out_toks = []

    def pre_final(xn_f):
        fstate["obs"] = [take([D], F32) for i in range(5)]
        fstate["xn"] = xn_f
        p.dma("sp", lambda e: e.dma_start(out=gbuf[:], in_=gains_d[3]), writes=["gbuf"], sem="gbuf")
        p.op("dve", lambda e: e.memset(st[:, 0, :], 0.0), writes=["ss"])

    def final_tb(tb):
        obs = fstate["obs"]
        jb = fstate["xn"][tb % 2]
        sl = tb % 5
        p.op("act", lambda e, tb=tb, jb=jb: e.activation(out=jb[:], in_=h[:, tb, :], func=AF.Square,
                                                         accum_out=st[:, 0, tb:tb + 1]),
             reads=[("h", tb), "ss"], writes=[("ss", tb), ("xn", tb % 2)])
        p.op("dve", lambda e, tb=tb: e.tensor_scalar(out=st[:, 1, tb:tb + 1], in0=st[:, 0, tb:tb + 1], scalar1=1.0 / D, scalar2=EPS,
                                                     op0=ALU.mult, op1=ALU.add), reads=[("ss", tb)], writes=[("ms", tb)])
        p.op("act", lambda e, tb=tb: e.activation(out=st[:, 2, tb:tb + 1], in_=st[:, 1, tb:tb + 1], func=AF.Sqrt),
             reads=[("ms", tb)], writes=[("sd", tb)])
        p.op("dve", lambda e, tb=tb: e.reciprocal(out=st[:, 3, tb:tb + 1], in_=st[:, 2, tb:tb + 1]), reads=[("sd", tb)], writes=[("rstd", tb)])
        p.op("dve", lambda e, tb=tb, sl=sl: e.scalar_tensor_tensor(
            out=obs[sl], in0=h[:, tb, :], scalar=st[:, 3, tb:tb + 1], in1=gbuf[:], op0=ALU.mult, op1=ALU.mult),
            reads=[("h", tb), ("rstd", tb), "gbuf"], writes=[("ob", sl)])
        out_toks.append(p.dma("sp", lambda e, tb=tb, sl=sl: e.dma_start(out=out_d[tb * 128:(tb + 1) * 128, :], in_=obs[sl]),
                              reads=[("ob", sl)], sem=("ob", sl)))

    if stop_after == "ffn2":
        ffn(1)
        return finish(dump_h("e"))
    ffn(1, on_final=final_tb, pre_final=pre_final)
    p.wait_all("sp", out_toks)
    p.wait_all("act", out_toks)
    p.emit(nc)
    return nc


def _consts():
    bf = ml_dtypes.bfloat16
    i = np.arange(128)
    cb = np.zeros((128, 6, 128), np.float32)
    cb[:, 0, :] = np.eye(128)
    cb[:, 1, :] = (i[:, None] >= i[None, :])
    cb[:, 2, :] = 1.0
    cb[:, 3, :] = (i[:, None] <= i[None, :])
    cb[:, 4, :] = (i[:, None] < i[None, :])
    return cb.astype(bf)


def make_in_maps(inputs):
    f = lambda a: np.ascontiguousarray(np.asarray(a, dtype=np.float32))
    x = f(inputs["x"])
    gains = np.stack([np.broadcast_to(f(inputs[k]).reshape(-1), (128, D)) for k in
                      ("ffn1_norm", "mix_norm", "ffn2_norm", "final_norm")]).astype(np.float32)
    w_in = f(inputs["w_in"])[0]
    cbf = _consts()
    lamv = np.stack([np.broadcast_to(f(inputs[k])[0], (128, 64)) for k in
                     ("lambda_q1", "lambda_k1", "lambda_q2", "lambda_k2")], axis=1).astype(np.float32)
    subln = f(inputs["diff_subln"])[0].reshape(128, 1)
    bg = f(inputs["b_gate"])[0].reshape(16, 128).T.copy()
    slopes = 2.0 ** (-8.0 * np.arange(1, 5) / 4.0)
    maps = []
    shared = dict(
        gains=gains, wg1=f(inputs["ffn1_w_gate"])[0], wu1=f(inputs["ffn1_w_up"])[0], wd1=f(inputs["ffn1_w_down"])[0],
        wg2=f(inputs["ffn2_w_gate"])[0], wu2=f(inputs["ffn2_w_up"])[0], wd2=f(inputs["ffn2_w_down"])[0],
        wgate=np.ascontiguousarray(w_in[:, 3072:5120]), bgate=bg, wa=f(inputs["w_branch_diff"])[0],
        wb=f(inputs["w_branch_sb"])[0], wout=f(inputs["w_out"])[0], lamv=lamv, subln=subln, cbf=cbf)
    pidx = np.arange(128, dtype=np.float64)[:, None]
    jj = np.arange(-28, 4, dtype=np.float64)[None, :]
    for c in range(8):
        b, r = c // 2, c % 2
        cols = []
        for base in (0, 512, 1024, 1536, 2048, 2560):
            cols.append(w_in[:, base + r * 256: base + (r + 1) * 256])
        wqkv = np.ascontiguousarray(np.concatenate(cols, axis=1))
        ab = np.stack([slopes[2 * r + hl] * (pidx + 128.0 * jj - 256.0) for hl in range(2)], axis=1).astype(np.float32)
        sel = np.zeros((128, 2), np.float32)
        sel[:, r] = 1.0
        m = dict(shared)
        m.update(x=np.ascontiguousarray(x[b, r * T:(r + 1) * T, :]), wqkv=wqkv, abias=ab, sel=sel)
        maps.append(m)
    return maps


_CACHE = {}


def kernel(**inputs):
    if "nc" not in _CACHE:
        _CACHE["nc"] = build_program()
    nc = _CACHE["nc"]
    maps = make_in_maps(inputs)
    res = run_bass_kernel_spmd(nc, maps, core_ids=list(range(8)))
    out = np.empty((4, S, D), np.float32)
    for c in range(8):
        b, r = c // 2, c % 2
        out[b, r * T:(r + 1) * T, :] = res.results[c]["out"]
    return out
```

```python
import os
import numpy as np
import ml_dtypes
import concourse.bass as bass
import concourse.mybir as mybir
from concourse.bass_utils import run_bass_kernel_spmd

F32 = mybir.dt.float32
BF16 = mybir.dt.bfloat16
AF = mybir.ActivationFunctionType
ALU = mybir.AluOpType

ENGS = ("pe", "act", "dve", "pool", "sp")
D = 1024
DFF = 2816
S = 4096
T = 2048
NTB = 16
NTT = 4
PARTS = [(0, 6), (6, 6), (12, 5), (17, 5)]
EPS = 1e-5
LAM_INIT = 0.8 - 0.6 * 1.0
PAIRS = [[0, 1], [2, 3], [4, 5], [6, 7]]
NJ = 32


def _freeze(fn):
    import types
    if fn is None or fn.__closure__ is None:
        return fn
    cells = []
    for c in fn.__closure__:
        try:
            cells.append(types.CellType(c.cell_contents))
        except ValueError:
            cells.append(c)
    return types.FunctionType(fn.__code__, fn.__globals__, fn.__name__, fn.__defaults__, tuple(cells))


class Prog:
    def __init__(self):
        self.ops = []
        self.eng_ops = {e: [] for e in ENGS}
        self.lastw = {}
        self.readers = {}
        self.sem_counts = {}
        self.sem_inc = {}
        self.last_c = {}
        self.dma_since = []

    def _collect(self, reads, writes, tok):
        deps = []
        for r in reads:
            t = self.lastw.get(r)
            if t is not None:
                deps.append(t)
        for w in writes:
            t = self.lastw.get(w)
            if t is not None:
                deps.append(t)
            deps.extend(self.readers.get(w, ()))
        for w in writes:
            self.lastw[w] = tok
            self.readers[w] = []
        for r in reads:
            if r in writes:
                continue
            self.readers.setdefault(r, []).append(tok)
        return [d for d in deps if d != tok]

    def op(self, eng, fn, reads=(), writes=()):
        idx = len(self.eng_ops[eng])
        tok = ("c", eng, idx)
        deps = self._collect(tuple(reads), tuple(writes), tok)
        o = dict(eng=eng, fn=_freeze(fn), deps=deps, kind="c", tok=tok)
        self.eng_ops[eng].append(o)
        self.ops.append(o)
        self.last_c[eng] = tok
        return tok

    def dma(self, eng, fn, reads=(), writes=(), sem=None, inc=16):
        n = self.sem_counts.get(sem, 0) + 1
        self.sem_counts[sem] = n
        self.sem_inc[sem] = inc
        tok = ("d", sem, n)
        deps = self._collect(tuple(reads), tuple(writes), tok)
        o = dict(eng=eng, fn=_freeze(fn), deps=deps, kind="d", tok=tok, sem=sem, inc=inc)
        self.eng_ops[eng].append(o)
        self.ops.append(o)
        self.dma_since.append(tok)
        return tok

    def wait_all(self, eng, toks):
        o = dict(eng=eng, fn=None, deps=list(toks), kind="w", tok=None)
        self.eng_ops[eng].append(o)
        self.ops.append(o)

    def barrier(self, carry=()):
        carry = dict(carry)
        skip = set(carry.values())
        toks = list(self.last_c.values()) + [t for t in self.dma_since if t not in skip]
        self.dma_since = [t for t in self.dma_since if t in skip]
        for e in ENGS:
            self.wait_all(e, toks)
        self.lastw = {}
        self.readers = {}
        for k, t in carry.items():
            self.lastw[k] = t

    def emit(self, nc):
        tokvc = {}
        cur = {e: {} for e in ENGS}
        milestones = set()

        def covered(clock, t):
            if t[0] == "c":
                return clock.get(("c", t[1]), -1) >= t[2]
            return clock.get(("d", t[1]), 0) >= t[2]

        def merge(clock, other):
            for k, v in other.items():
                if clock.get(k, -1) < v:
                    clock[k] = v

        for o in self.ops:
            e = o["eng"]
            clock = cur[e]
            waits = []
            for t in o["deps"]:
                if t[0] == "c" and t[1] == e and e == "pe":
                    continue
                if covered(clock, t):
                    continue
                waits.append(t)
                merge(clock, tokvc[t])
                if t[0] == "c":
                    milestones.add(t)
            best = {}
            for t in waits:
                k = (t[0], t[1])
                if k not in best or best[k][2] < t[2]:
                    best[k] = t
            o["waits"] = list(best.values())
            if o["tok"] is not None:
                know = dict(clock)
                t = o["tok"]
                k = (t[0], t[1])
                know[k] = max(know.get(k, -1), t[2])
                tokvc[t] = know
        msval = {}
        for e in ENGS:
            c = 0
            for o in self.eng_ops[e]:
                if o["kind"] == "c" and o["tok"] in milestones:
                    c += 1
                    msval[o["tok"]] = c
        esem = {e: nc.alloc_semaphore(name=f"es_{e}") for e in ENGS}
        dsem = {k: nc.alloc_semaphore(name=f"ds_{i}") for i, k in enumerate(self.sem_counts)}
        self.n_sems = len(esem) + len(dsem)

        def run(e, eng):
            for o in self.eng_ops[e]:
                for t in o["waits"]:
                    if t[0] == "c":
                        eng.wait_ge(esem[t[1]], msval[t])
                    else:
                        eng.wait_ge(dsem[t[1]], t[2] * self.sem_inc[t[1]])
                if o["fn"] is None:
                    continue
                ins = o["fn"](eng)
                if o["kind"] == "c":
                    if o["tok"] in milestones:
                        ins.then_inc(esem[e], 1)
                else:
                    ins.then_inc(dsem[o["sem"]], o["inc"])

        with nc.Block() as block:
            @block.tensor
            def _(eng):
                run("pe", eng)

            @block.scalar
            def _(eng):
                run("act", eng)

            @block.vector
            def _(eng):
                run("dve", eng)

            @block.gpsimd
            def _(eng):
                run("pool", eng)

            @block.sync
            def _(eng):
                run("sp", eng)


def build_program(stop_after=None, dbg=False, lite=False):
    nc = bass.Bass("TRN2", target_bir_lowering=False)
    p = Prog()

    def din(name, shape, dt=F32):
        if lite and name[0] == "w":
            shape = [128, 128]
        return nc.dram_tensor(name, list(shape), dt, kind="ExternalInput").ap()

    x_d = din("x", [T, D])
    gains_d = din("gains", [4, 128, D])
    wg_d = [din("wg1", [D, DFF]), din("wg2", [D, DFF])]
    wu_d = [din("wu1", [D, DFF]), din("wu2", [D, DFF])]
    wd_d = [din("wd1", [DFF, D]), din("wd2", [DFF, D])]
    wqkv_d = din("wqkv", [D, 1536])
    wgate_d = din("wgate", [D, 2048])
    bgate_d = din("bgate", [128, 16])
    wa_d = din("wa", [512, D])
    wb_d = din("wb", [512, D])
    wout_d = din("wout", [D, D])
    lamv_d = din("lamv", [128, 4, 64])
    subln_d = din("subln", [128, 1])
    abias_d = din("abias", [128, 2, NJ])
    sel_d = din("sel", [128, 2])
    cb_d = din("cbf", [128, 6, 128], BF16)
    out_d = nc.dram_tensor("out", [T, D], F32, kind="ExternalOutput").ap()
    if dbg:
        dbg_h = nc.dram_tensor("dbg_h", [T, D], F32, kind="ExternalOutput").ap()
        dbg_att = nc.dram_tensor("dbg_att", [512, S], BF16, kind="ExternalOutput").ap()
    ag1_in = [nc.dram_tensor(f"ag1_in{k}", [D, T // 2], BF16) for k in range(2)]
    ag1_out = [nc.dram_tensor(f"ag1_out{k}", [2 * D, T // 2], BF16) for k in range(2)]
    ag2_in = [nc.dram_tensor(f"ag2_in{k}", [256, S], BF16) for k in range(2)]
    ag2_out = [nc.dram_tensor(f"ag2_out{k}", [512, S], BF16) for k in range(2)]

    h = nc.alloc_sbuf_tensor("h", [128, NTB, D], F32)
    cb = nc.alloc_sbuf_tensor("cb", [128, 6, 128], BF16)
    ident, tri, ones, mda, msb, zer = (cb[:, i, :] for i in range(6))
    gbuf = nc.alloc_sbuf_tensor("gbuf", [128, D], F32)
    st = nc.alloc_sbuf_tensor("st", [128, 4, NTB], F32)
    small = nc.alloc_sbuf_tensor("small", [128, 16], F32)
    ARENA = 66560
    arena = nc.alloc_sbuf_tensor("arena", [128, ARENA], BF16)
    aoff = [0]

    def areset():
        aoff[0] = 0

    def take(shape, dt):
        n = 1
        for d_ in shape:
            n *= d_
        ne = n * (2 if dt == F32 else 1)
        ne = (ne + 15) // 16 * 16
        assert aoff[0] + ne <= ARENA, (aoff[0], ne)
        v = arena[:, aoff[0]:aoff[0] + ne]
        aoff[0] += ne
        if dt == F32:
            v = v.bitcast(F32)
        v = v[:, 0:n]
        if len(shape) == 2:
            v = v.rearrange("p (a b) -> p a b", a=shape[0])
        elif len(shape) == 3:
            v = v.rearrange("p (a b c) -> p a b c", a=shape[0], b=shape[1])
        return v

    pp = [nc.alloc_psum_tensor(f"pp{i}", [128, 2, 512], F32) for i in range(4)]
    ps = [pp[i // 2][:, i % 2, :] for i in range(8)]
    psb = [t.bitcast(BF16) for t in ps]

    def PS(i):
        return ("ps", i)

    tx = []
    for q in range(4):
        tx.append(p.dma("sp", lambda e, q=q: e.dma_start(
            out=h[:, q * 4:(q + 1) * 4, :],
            in_=x_d[q * 512:(q + 1) * 512, :].rearrange("(tb p) d -> p tb d", p=128)),
            writes=[("h", q * 4 + i) for i in range(4)], sem=("hload", q)))
    p.dma("sp", lambda e: e.dma_start(out=cb[:], in_=cb_d), writes=["cb"], sem="cb")

    def norm_to_T(gi, dstT, xn, junk, on_tb=None):
        p.dma("sp", lambda e: e.dma_start(out=gbuf[:], in_=gains_d[gi]), writes=["gbuf"], sem="gbuf")
        p.op("dve", lambda e: e.memset(st[:, 0, :], 0.0), writes=["ss"])
        for tb in range(NTB):
            jb = xn[tb % 2]
            p.op("act", lambda e, tb=tb, jb=jb: e.activation(out=jb[:], in_=h[:, tb, :], func=AF.Square,
                                                             accum_out=st[:, 0, tb:tb + 1]),
                 reads=[("h", tb), "ss"], writes=[("ss", tb), ("xn", tb % 2)])
        p.op("dve", lambda e: e.tensor_scalar(out=st[:, 1, :], in0=st[:, 0, :], scalar1=1.0 / D, scalar2=EPS,
                                              op0=ALU.mult, op1=ALU.add), reads=[("ss", i) for i in range(NTB)], writes=["ms"])
        p.op("act", lambda e: e.activation(out=st[:, 2, :], in_=st[:, 1, :], func=AF.Sqrt), reads=["ms"], writes=["sd"])
        p.op("dve", lambda e: e.reciprocal(out=st[:, 3, :], in_=st[:, 2, :]), reads=["sd"], writes=["rstd"])
        import os
        NTBX = int(os.environ.get("NTBX", NTB))
        for tb in range(NTBX):
            sl = tb % 2
            if os.environ.get("NOSTT"):
                p.op("dve", lambda e, tb=tb, sl=sl: e.tensor_copy(out=xn[sl][:], in_=h[:, tb, :]),
                     reads=[("h", tb), "rstd", "gbuf"], writes=[("xn", sl)])
            else:
                p.op("dve", lambda e, tb=tb, sl=sl: e.scalar_tensor_tensor(
                    out=xn[sl][:], in0=h[:, tb, :], scalar=st[:, 3, tb:tb + 1], in1=gbuf[:],
                    op0=ALU.mult, op1=ALU.mult), reads=[("h", tb), "rstd", "gbuf"], writes=[("xn", sl)])
            bA, bB = (6, 7) if tb % 2 == 0 else (4, 5)
            for c in range(8):
                bank = bA if c < 4 else bB
                p.op("pe", lambda e, c=c, sl=sl, bank=bank: e.transpose(
                    out=psb[bank][:, (c % 4) * 128:(c % 4 + 1) * 128], in_=xn[sl][:, c * 128:(c + 1) * 128], identity=ident),
                    reads=[("xn", sl), "cb"], writes=[PS(bank)])
            srcA = psb[bA][:, 0:512].rearrange("p (c t) -> p c t", c=4)
            srcB = psb[bB][:, 0:512].rearrange("p (c t) -> p c t", c=4)
            p.op("act", lambda e, tb=tb, srcA=srcA: e.copy(out=dstT[:, 0:4, tb * 128:(tb + 1) * 128], in_=srcA),
                 reads=[PS(bA)], writes=[("xT", tb // 4, 0)])
            p.op("dve", lambda e, tb=tb, srcB=srcB: e.tensor_copy(out=dstT[:, 4:8, tb * 128:(tb + 1) * 128], in_=srcB),
                 reads=[PS(bB)], writes=[("xT", tb // 4, 1)])
            if on_tb is not None:
                on_tb(tb)

    def xT_keys(tt):
        return [("xT", tt, 0), ("xT", tt, 1)]

    def ffn(fi, on_final=None, pre_final=None):
        areset()
        xnT = take([8, T], BF16)
        actT = take([6, T], BF16)
        wdb = [take([6, D], BF16) for i in range(2)]
        wgb = [take([8, 256], BF16) for i in range(2)]
        wub = [take([8, 256], BF16) for i in range(2)]
        xn = [take([D], BF16) for i in range(2)]
        sg = [take([512], F32) for i in range(2)]
        junk = take([D], BF16)
        norm_to_T(0 if fi == 0 else 2, xnT, xn, junk)
        if stop_after == "norm1":
            p.barrier()
            return
        wgv = wg_d[fi].rearrange("(c p) f -> p c f", p=128)
        wuv = wu_d[fi].rearrange("(c p) f -> p c f", p=128)
        wdv = wd_d[fi].rearrange("(c p) m -> p c m", p=128)
        gi = 0
        it = 0
        dn = 0
        for pi, (fc0, nfc) in enumerate(PARTS):
            if stop_after == "part1" and pi >= 1:
                break
            wsl = pi % 2
            p.dma("pool", lambda e, wsl=wsl, fc0=fc0, nfc=nfc: e.dma_start(
                out=wdb[wsl][:, 0:nfc, :], in_=wdv[:, fc0:fc0 + nfc, :]), writes=[("wd", wsl)], sem=("wd", fi, wsl))
            groups = []
            k = 0
            while k < nfc:
                n = min(2, nfc - k)
                groups.append((k, n))
                k += n
            for (k0, n) in groups:
                gs = gi % 2
                gi += 1
                col0 = (fc0 + k0) * 128
                p.dma("pool", lambda e, gs=gs, col0=col0, n=n: e.dma_start(
                    out=wgb[gs][:, :, 0:n * 128], in_=wgv[:, :, col0:col0 + n * 128]), writes=[("wg", gs)], sem=("wg", fi, gs))
                p.dma("pool", lambda e, gs=gs, col0=col0, n=n: e.dma_start(
                    out=wub[gs][:, :, 0:n * 128], in_=wuv[:, :, col0:col0 + n * 128]), writes=[("wu", gs)], sem=("wu", fi, gs))
                for jj in range(n):
                    fcl = k0 + jj
                    for tt in range(NTT):
                        b = it % 2
                        it += 1
                        for c in range(8):
                            p.op("pe", lambda e, gs=gs, jj=jj, c=c, tt=tt, b=b: e.matmul(
                                ps[b][:], lhsT=wgb[gs][:, c, jj * 128:(jj + 1) * 128], rhs=xnT[:, c, tt * 512:(tt + 1) * 512],
                                start=(c == 0), stop=(c == 7)), reads=[("wg", gs)] + xT_keys(tt), writes=[PS(b)])
                        for c in range(8):
                            p.op("pe", lambda e, gs=gs, jj=jj, c=c, tt=tt, b=b: e.matmul(
                                ps[2 + b][:], lhsT=wub[gs][:, c, jj * 128:(jj + 1) * 128], rhs=xnT[:, c, tt * 512:(tt + 1) * 512],
                                start=(c == 0), stop=(c == 7)), reads=[("wu", gs)] + xT_keys(tt), writes=[PS(2 + b)])
                        p.op("act", lambda e, b=b: e.activation(out=sg[b][:], in_=ps[b][:], func=AF.Silu),
                             reads=[PS(b)], writes=[("sg", b)])
                        p.op("dve", lambda e, b=b, fcl=fcl, tt=tt: e.tensor_tensor(
                            out=actT[:, fcl, tt * 512:(tt + 1) * 512], in0=sg[b][:], in1=ps[2 + b][:], op=ALU.mult),
                            reads=[("sg", b), PS(2 + b)], writes=[("actT", fcl, tt)])
            lastp = (pi == len(PARTS) - 1)
            if lastp and pre_final is not None:
                pre_final(xn)
            for tb in range(NTB):
                for mh in range(2):
                    b = 4 + dn % 2
                    dn += 1
                    for fcl in range(nfc):
                        p.op("pe", lambda e, fcl=fcl, tb=tb, mh=mh, b=b, wsl=wsl, nfc=nfc: e.matmul(
                            ps[b][:], lhsT=actT[:, fcl, tb * 128:(tb + 1) * 128], rhs=wdb[wsl][:, fcl, mh * 512:(mh + 1) * 512],
                            start=(fcl == 0), stop=(fcl == nfc - 1)),
                            reads=[("actT", fcl, tb // 4), ("wd", wsl)], writes=[PS(b)])
                    p.op("dve", lambda e, tb=tb, mh=mh, b=b: e.scalar_tensor_tensor(
                        out=h[:, tb, mh * 512:(mh + 1) * 512], in0=ps[b][:], scalar=0.5, in1=h[:, tb, mh * 512:(mh + 1) * 512],
                        op0=ALU.mult, op1=ALU.add), reads=[PS(b), ("h", tb)], writes=[("h", tb)])
                if lastp and on_final is not None:
                    on_final(tb)
        if fi == 0 or stop_after == "ffn2":
            p.barrier()
        return xn

    def dump_h(tag):
        toks = []
        for q in range(4):
            toks.append(p.dma("sp", lambda e, q=q: e.dma_start(
                out=dbg_h[q * 512:(q + 1) * 512, :].rearrange("(tb p) d -> p tb d", p=128), in_=h[:, q * 4:(q + 1) * 4, :]),
                reads=[("h", q * 4 + i) for i in range(4)], sem=("dbgh", tag, q)))
        return toks

    def finish(extra):
        p.wait_all("sp", extra)
        p.emit(nc)
        return nc

    if stop_after == "load":
        return finish(dump_h("l"))
    if not os.environ.get("SKIPFFN"):
        ffn(0)
    if stop_after in ("ffn1", "norm1", "part1"):
        return finish(dump_h("a"))

    areset()
    wqk = take([8, 768], BF16)
    wqv = wqkv_d.rearrange("(c p) f -> p c f", p=128)
    tok_wqk = p.dma("pool", lambda e: e.dma_start(out=wqk, in_=wqv[:, :, 0:768]), writes=["wqk"], sem="wqk")
    nT = take([8, T], BF16)
    xn0 = [take([D], BF16) for i in range(2)]
    junk0 = take([D], BF16)
    cc1_tok = {}

    def after_tb(tb):
        if tb % 8 != 7:
            return
        k = tb // 8
        for c in range(8):
            p.dma("sp", lambda e, c=c, k=k: e.dma_start(out=ag1_in[k].ap()[c * 128:(c + 1) * 128, :],
                                                       in_=nT[:, c, k * 1024:(k + 1) * 1024]),
                  reads=xT_keys(2 * k) + xT_keys(2 * k + 1), writes=[("ag1_in", k, c)], sem=("ag1w", k, c))
        cc1_tok[k] = p.dma("pool", lambda e, k=k: e.collective_compute("AllGather", ALU.bypass, replica_groups=PAIRS,
                                                                     ins=[ag1_in[k].ap().opt()], outs=[ag1_out[k].ap().opt()]),
                           reads=[("ag1_in", k, c) for c in range(8)], writes=[("ag1_out", k)], sem=("cc1", k), inc=1)

    norm_to_T(1, nT, xn0, junk0, on_tb=after_tb)
    lamv = nc.alloc_sbuf_tensor("lamv_sb", [128, 4, 64], F32)
    lprod = nc.alloc_sbuf_tensor("lprod_sb", [128, 2, 64], F32)
    subl = nc.alloc_sbuf_tensor("subl_sb", [128, 2], F32)
    abias = nc.alloc_sbuf_tensor("abias_sb", [128, 2, NJ], F32)
    sel = nc.alloc_sbuf_tensor("sel_sb", [128, 2], F32)
    bgate = nc.alloc_sbuf_tensor("bgate_sb", [128, 16], F32)
    p.dma("sp", lambda e: e.dma_start(out=lamv[:], in_=lamv_d), writes=["lamv"], sem="lamv")
    p.dma("sp", lambda e: e.dma_start(out=subl[:, 0:1], in_=subln_d), writes=["subl0"], sem="subl")
    p.dma("sp", lambda e: e.dma_start(out=abias[:], in_=abias_d), writes=["abias"], sem="abias")
    p.dma("sp", lambda e: e.dma_start(out=sel[:], in_=sel_d), writes=["sel"], sem="sel")
    p.dma("sp", lambda e: e.dma_start(out=bgate[:], in_=bgate_d), writes=["bgate"], sem="bgate")
    p.op("dve", lambda e: e.tensor_tensor(out=lprod[:, 0, :], in0=lamv[:, 0, :], in1=lamv[:, 1, :], op=ALU.mult),
         reads=["lamv"], writes=["lprod0"])
    p.op("dve", lambda e: e.tensor_tensor(out=lprod[:, 1, :], in0=lamv[:, 2, :], in1=lamv[:, 3, :], op=ALU.mult),
         reads=["lamv"], writes=["lprod1"])
    p.op("dve", lambda e: e.reduce_sum(out=small[:, 0:2], in_=lprod[:], axis=mybir.AxisListType.X),
         reads=["lprod0", "lprod1"], writes=["sm01"])
    p.op("act", lambda e: e.activation(out=small[:, 2:4], in_=small[:, 0:2], func=AF.Exp), reads=["sm01"], writes=["sm23"])
    p.op("dve", lambda e: e.tensor_tensor(out=small[:, 4:5], in0=small[:, 3:4], in1=small[:, 2:3], op=ALU.subtract),
         reads=["sm23"], writes=["sm4"])
    p.op("dve", lambda e: e.tensor_scalar(out=small[:, 5:6], in0=small[:, 4:5], scalar1=-LAM_INIT, scalar2=0.0,
                                          op0=ALU.add, op1=ALU.add), reads=["sm4"], writes=["neglam"])
    p.op("dve", lambda e: e.tensor_scalar(out=subl[:, 1:2], in0=subl[:, 0:1], scalar1=1.0 - LAM_INIT, scalar2=0.0,
                                          op0=ALU.mult, op1=ALU.add), reads=["subl0"], writes=["subl1"])
    neglam = small[:, 5:6]
    carry = {("ag1_out", k): cc1_tok[k] for k in range(2)}
    carry["wqk"] = tok_wqk
    p.barrier(carry=carry)
    if stop_after == "ag1":
        return finish(dump_h("b"))

    areset()
    wqk = take([8, 768], BF16)
    nts = [take([8, 512], BF16) for i in range(2)]
    QT = take([2, S], BF16)
    KT = take([2, S], BF16)
    V = take([32, 256], BF16)
    Pp = [take([2, 512], BF16) for i in range(3)]
    Pm = [[Pp[i][:, m, :] for i in range(3)] for m in range(2)]
    fin = [take([512], F32) for i in range(4)]
    fin2 = [take([512], F32) for i in range(4)]
    sqb = take([512], BF16)
    sqb2 = take([512], BF16)
    acc0 = [take([512], F32) for i in range(2)]
    accb = [take([512], BF16) for i in range(2)]
    aout = [take([512], BF16) for i in range(2)]
    Eb = [take([512], F32) for i in range(4)]
    Ub = [take([512], BF16) for i in range(2)]
    Xb = [take([512], F32) for i in range(2)]
    Ab = [take([512], BF16) for i in range(4)]
    Usum2 = [take([512], BF16) for i in range(2)]
    ag1v = [ag1_out[k].ap().rearrange("(r c p) t -> p r c t", r=2, c=8, p=128) for k in range(2)]
    aoi = [0]

    def project(br):
        qscale = 1.0 if br == 0 else 128.0 ** -0.5
        for ti, t8 in enumerate([0, 1, 4, 5, 2, 3, 6, 7]):
            r, tl = t8 // 4, t8 % 4
            sl = ti % 2
            piece, tl2 = tl // 2, tl % 2
            p.dma("sp", lambda e, r=r, tl2=tl2, sl=sl, piece=piece: e.dma_start(
                out=nts[sl][:], in_=ag1v[piece][:, r, :, tl2 * 512:(tl2 + 1) * 512]),
                reads=[("ag1_out", piece)], writes=[("nts", sl, 0), ("nts", sl, 1)], sem=("nts", sl))
            cnt = 0
            for which in range(2):
                for hl in range(2):
                    b = cnt % 2
                    cnt += 1
                    col = which * 256 + hl * 128
                    for c in range(8):
                        p.op("pe", lambda e, c=c, sl=sl, col=col, b=b: e.matmul(
                            ps[b][:], lhsT=wqk[:, c, col:col + 128], rhs=nts[sl][:, c, :], start=(c == 0), stop=(c == 7)),
                            reads=["wqk", ("nts", sl, 0), ("nts", sl, 1)], writes=[PS(b)])
                    if which == 0:
                        p.op("act", lambda e, hl=hl, t8=t8, b=b: e.mul(out=QT[:, hl, t8 * 512:(t8 + 1) * 512], in_=ps[b][:], mul=qscale),
                             reads=[PS(b)], writes=[("QT", hl, t8)])
                    else:
                        p.op("dve", lambda e, hl=hl, t8=t8, b=b: e.tensor_copy(out=KT[:, hl, t8 * 512:(t8 + 1) * 512], in_=ps[b][:]),
                             reads=[PS(b)], writes=[("KT", hl, t8)])
            for tb4 in range(4):
                b = 2 + tb4 % 2
                for c in range(8):
                    p.op("pe", lambda e, c=c, sl=sl, tb4=tb4, b=b: e.matmul(
                        ps[b][:, 0:256], lhsT=nts[sl][:, c, tb4 * 128:(tb4 + 1) * 128], rhs=wqk[:, c, 512:768],
                        start=(c == 0), stop=(c == 7)), reads=["wqk", ("nts", sl, 0), ("nts", sl, 1)], writes=[PS(b)])
                if tb4 % 2 == 0:
                    p.op("act", lambda e, t8=t8, tb4=tb4, b=b: e.copy(out=V[:, t8 * 4 + tb4, :], in_=ps[b][:, 0:256]),
                         reads=[PS(b)], writes=[("V", t8)])
                else:
                    p.op("dve", lambda e, t8=t8, tb4=tb4, b=b: e.tensor_copy(out=V[:, t8 * 4 + tb4, :], in_=ps[b][:, 0:256]),
                         reads=[PS(b)], writes=[("V", t8)])

    def store_att(br, hl, qt, src_sl):
        row0 = hl * 128
        p.dma("sp", lambda e: e.dma_start(out=ag2_in[br].ap()[row0:row0 + 128, qt * 512:(qt + 1) * 512], in_=aout[src_sl][:]),
              reads=[("aout", src_sl)], writes=[("ag2_in", br, hl, qt)], sem=("aout", src_sl))

    def run_pipeline(nblocks, stages, skews, deferred):
        last = nblocks + max(skews)
        it = 0
        while it < last or any(k >= it for k in deferred):
            for st_fn, sk in zip(stages, skews):
                g = it - sk
                if 0 <= g < nblocks:
                    st_fn(g)
            for fn in deferred.pop(it, []):
                fn()
            it += 1
            if it > last + 64:
                break
        for k in sorted(deferred):
            for fn in deferred[k]:
                fn()
        deferred.clear()

    def attn_da():
        blocks = [(hl, qt, kb) for hl in range(2) for qt in range(8) for kb in range(4 * qt + 4)]
        NB = len(blocks)
        deferred = {}
        fsets = [fin, fin2]
        sqbs = [sqb, sqb2]

        def geom(g):
            hl, qt, kb = blocks[g]
            j = kb - 4 * qt
            return hl, qt, kb, j, (128 * j if j > 0 else 0)

        def stage_scores(g):
            hl, qt, kb, j, c0 = geom(g)
            par = g % 2
            p3 = g % 3
            for m in range(2):
                bank = par * 2 + m
                p.op("pe", lambda e, m=m, kb=kb, c0=c0, bank=bank, hl=hl, qt=qt: e.matmul(
                    ps[bank][:, c0:512], lhsT=KT[64 * m:64 * m + 64, hl, kb * 128:(kb + 1) * 128],
                    rhs=QT[64 * m:64 * m + 64, hl, qt * 512 + c0:(qt + 1) * 512], start=True, stop=True),
                    reads=[("KT", hl, kb // 4), ("QT", hl, qt)], writes=[PS(bank)])
            p.op("act", lambda e, c0=c0, par=par, j=j, hl=hl, p3=p3: e.activation(
                out=Pp[p3][:, :, c0:512], in_=pp[par][:, :, c0:512], func=AF.Exp,
                bias=abias[:, hl, j + 28:j + 29], scale=0.125),
                reads=[PS(par * 2), PS(par * 2 + 1), "abias"], writes=[("P", 0, p3), ("P", 1, p3)])
            if j >= 0:
                for m in range(2):
                    p.op("dve", lambda e, m=m, c0=128 * j, p3=p3: e.tensor_tensor(
                        out=Pm[m][p3][:, c0:c0 + 128], in0=Pm[m][p3][:, c0:c0 + 128], in1=mda, op=ALU.mult),
                        reads=[("P", m, p3), "cb"], writes=[("P", m, p3)])

        def stage_acc(g):
            hl, qt, kb, j, c0 = geom(g)
            par = g % 2
            p3 = g % 3
            nkb = 4 * qt + 4
            for m in range(2):
                p.op("pe", lambda e, m=m, kb=kb, c0=c0, p3=p3, hl=hl, nkb=nkb: e.matmul(
                    ps[4 + m][:, c0:512], lhsT=V[:, kb, hl * 128:(hl + 1) * 128], rhs=Pm[m][p3][:, c0:512],
                    start=(kb == 0), stop=(kb == nkb - 1)),
                    reads=[("V", kb // 4), ("P", m, p3)], writes=[PS(4 + m)])
                if m == 1:
                    p.op("pe", lambda e, m=m, c0=c0, p3=p3, kb=kb, nkb=nkb: e.matmul(
                        ps[6 + m][:, c0:512], lhsT=ones, rhs=Pm[m][p3][:, c0:512],
                        start=(kb == 0), stop=(kb == nkb - 1)),
                        reads=["cb", ("P", m, p3)], writes=[PS(6 + m)])
                else:
                    tpa = (hl * 8 + qt) % 2
                    if kb == 0:
                        p.op("dve", lambda e, p3=p3, tpa=tpa: e.tensor_copy(out=acc0[tpa][:], in_=Pm[0][p3][:]),
                             reads=[("P", 0, p3)], writes=[("acc0", tpa)])
                    else:
                        p.op("dve", lambda e, p3=p3, tpa=tpa, c0=c0: e.tensor_tensor(
                            out=acc0[tpa][:, c0:512], in0=acc0[tpa][:, c0:512], in1=Pm[0][p3][:, c0:512], op=ALU.add),
                            reads=[("P", 0, p3), ("acc0", tpa)], writes=[("acc0", tpa)])
            if kb == nkb - 1:
                tp = (hl * 8 + qt) % 2
                l0s, l1s, o0s, o1s = fsets[tp]
                sq = sqbs[tp]
                K = lambda n: (n, tp)
                p.op("dve", lambda e, tp=tp: e.tensor_copy(out=accb[tp][:], in_=acc0[tp][:]), reads=[("acc0", tp)], writes=[("accb", tp)])
                p.op("pe", lambda e, tp=tp: e.matmul(ps[6][:], lhsT=ones, rhs=accb[tp][:], start=True, stop=True),
                     reads=["cb", ("accb", tp)], writes=[PS(6)])
                p.op("act", lambda e, l1s=l1s: e.copy(out=l1s[:], in_=ps[7][:]), reads=[PS(7)], writes=[K("l1s")])
                p.op("dve", lambda e, o0s=o0s: e.tensor_copy(out=o0s[:], in_=ps[4][:]), reads=[PS(4)], writes=[K("o0s")])
                p.op("dve", lambda e, o1s=o1s: e.tensor_copy(out=o1s[:], in_=ps[5][:]), reads=[PS(5)], writes=[K("o1s")])
                p.op("act", lambda e, l0s=l0s: e.copy(out=l0s[:], in_=ps[6][:]), reads=[PS(6)], writes=[K("l0s")])
                it_now = g + 1
                steps = [
                    lambda: p.op("dve", lambda e: e.reciprocal(out=l0s[:], in_=l0s[:]), reads=[K("l0s")], writes=[K("l0s")]),
                    lambda: p.op("dve", lambda e: e.reciprocal(out=l1s[:], in_=l1s[:]), reads=[K("l1s")], writes=[K("l1s")]),
                    lambda: p.op("dve", lambda e: e.tensor_tensor(out=o0s[:], in0=o0s[:], in1=l0s[:], op=ALU.mult),
                                 reads=[K("o0s"), K("l0s")], writes=[K("o0s")]),
                    lambda: p.op("dve", lambda e: e.tensor_tensor(out=o1s[:], in0=o1s[:], in1=l1s[:], op=ALU.mult),
                                 reads=[K("o1s"), K("l1s")], writes=[K("o1s")]),
                    lambda: p.op("dve", lambda e: e.scalar_tensor_tensor(out=o0s[:], in0=o1s[:], scalar=neglam, in1=o0s[:],
                                                                         op0=ALU.mult, op1=ALU.add),
                                 reads=[K("o1s"), K("o0s"), "neglam"], writes=[K("o0s")]),
                    lambda: p.op("act", lambda e: e.activation(out=sq[:], in_=o0s[:], func=AF.Square), reads=[K("o0s")], writes=[K("sq")]),
                    lambda: (p.op("pe", lambda e: e.matmul(ps[6][:], lhsT=ones, rhs=sq[:], start=True, stop=True),
                                  reads=["cb", K("sq")], writes=[PS(6)]),
                             p.op("dve", lambda e: e.tensor_scalar(out=l0s[:], in0=ps[6][:], scalar1=1.0 / 128, scalar2=EPS,
                                                                   op0=ALU.mult, op1=ALU.add), reads=[PS(6)], writes=[K("l0s")])),
                    lambda: p.op("act", lambda e: e.activation(out=l1s[:], in_=l0s[:], func=AF.Ln), reads=[K("l0s")], writes=[K("l1s")]),
                    lambda: p.op("act", lambda e: e.activation(out=l0s[:], in_=l1s[:], func=AF.Exp, scale=-0.5), reads=[K("l1s")], writes=[K("l0s")]),
                ]

                def last_step(hl=hl, qt=qt, o0s=o0s, l0s=l0s, K=K):
                    sl = aoi[0] % 2
                    aoi[0] += 1
                    p.op("dve", lambda e, sl=sl: e.scalar_tensor_tensor(out=aout[sl][:], in0=o0s[:], scalar=subl[:, 1:2], in1=l0s[:],
                                                                        op0=ALU.mult, op1=ALU.mult),
                         reads=[K("o0s"), K("l0s"), "subl1"], writes=[("aout", sl)])
                    store_att(0, hl, qt, sl)
                steps.append(last_step)
                for si, fn in enumerate(steps):
                    deferred.setdefault(it_now + 1 + si, []).append(fn)

        run_pipeline(NB, [stage_scores, stage_acc], [0, 1], deferred)

    def attn_sb():
        blocks = []
        for hl in range(2):
            for qt in range(8):
                nkb = 4 * qt + 4
                for i, kb in enumerate(range(nkb - 1, -1, -1)):
                    blocks.append((hl, qt, kb, i, nkb))
        NB = len(blocks)
        deferred = {}

        def geom(g):
            hl, qt, kb, i, nkb = blocks[g]
            j = kb - 4 * qt
            tile = hl * 8 + qt
            return hl, qt, kb, i, nkb, j, (128 * j if j > 0 else 0), tile

        def st1(g):
            hl, qt, kb, i, nkb, j, c0, tile = geom(g)
            par = g % 2
            e3 = g % 4
            p.op("pe", lambda e, kb=kb, c0=c0, par=par, hl=hl, qt=qt: e.matmul(
                ps[par][:, c0:512], lhsT=KT[:, hl, kb * 128:(kb + 1) * 128],
                rhs=QT[:, hl, qt * 512 + c0:(qt + 1) * 512], start=True, stop=True),
                reads=[("KT", hl, kb // 4), ("QT", hl, qt)], writes=[PS(par)])
            p.op("act", lambda e, c0=c0, par=par, e3=e3: e.activation(out=Eb[e3][:, c0:512], in_=ps[par][:, c0:512], func=AF.Exp),
                 reads=[PS(par)], writes=[("E", e3)])

        def st1b(g):
            hl, qt, kb, i, nkb, j, c0, tile = geom(g)
            par = g % 2
            e3 = g % 4
            p.op("act", lambda e, c0=c0, par=par, e3=e3: e.activation(out=Ub[par][:, c0:512], in_=Eb[e3][:, c0:512], func=AF.Ln, bias=1.0),
                 reads=[("E", e3)], writes=[("U", par)])
            if j >= 0:
                p.op("dve", lambda e, c0=128 * j, par=par: e.tensor_tensor(
                    out=Ub[par][:, c0:c0 + 128], in0=Ub[par][:, c0:c0 + 128], in1=msb, op=ALU.mult),
                    reads=[("U", par), "cb"], writes=[("U", par)])

        def st2(g):
            hl, qt, kb, i, nkb, j, c0, tile = geom(g)
            par = g % 2
            e3 = g % 4
            us_r = Usum2[i % 2]
            us_w = Usum2[(i + 1) % 2]
            kr = ("usum", i % 2)
            kw = ("usum", (i + 1) % 2)
            first = (i == 0)
            p.op("pe", lambda e, c0=c0, par=par, first=first: e.matmul(
                ps[2 + par][:, c0:512], lhsT=tri, rhs=Ub[par][:, c0:512], start=True, stop=first),
                reads=["cb", ("U", par)], writes=[PS(2 + par)])
            if not first:
                p.op("pe", lambda e, c0=c0, par=par, us_r=us_r: e.matmul(
                    ps[2 + par][:, c0:512], lhsT=ones, rhs=us_r[:, c0:512], start=False, stop=True),
                    reads=["cb", kr], writes=[PS(2 + par)])
            if i < nkb - 1:
                if first:
                    p.op("pool", lambda e, us_w=us_w: e.memset(us_w[:], 0.0), writes=[kw])
                    p.op("pool", lambda e, c0=c0, par=par, us_w=us_w: e.tensor_copy(out=us_w[:, c0:512], in_=Ub[par][:, c0:512]),
                         reads=[("U", par)], writes=[kw])
                else:
                    if c0 > 0:
                        p.op("pool", lambda e, c0=c0, us_w=us_w, us_r=us_r: e.tensor_copy(out=us_w[:, 0:c0], in_=us_r[:, 0:c0]),
                             reads=[kr], writes=[kw])
                    p.op("pool", lambda e, c0=c0, par=par, us_w=us_w, us_r=us_r: e.tensor_tensor(
                        out=us_w[:, c0:512], in0=us_r[:, c0:512], in1=Ub[par][:, c0:512], op=ALU.add),
                        reads=[kr, ("U", par)], writes=[kw])

        def st2b(g):
            hl, qt, kb, i, nkb, j, c0, tile = geom(g)
            par = g % 2
            e3 = g % 4
            p.op("act", lambda e, c0=c0, par=par: e.activation(out=Xb[par][:, c0:512], in_=ps[2 + par][:, c0:512],
                                                             func=AF.Exp, scale=-1.0),
                 reads=[PS(2 + par)], writes=[("X", par)])
            p.op("dve", lambda e, c0=c0, par=par, e3=e3: e.tensor_tensor(
                out=Ab[e3][:, c0:512], in0=Eb[e3][:, c0:512], in1=Xb[par][:, c0:512], op=ALU.mult),
                reads=[("E", e3), ("X", par)], writes=[("A", e3)])
            if j >= 0:
                p.op("dve", lambda e, c0=128 * j, e3=e3: e.tensor_tensor(
                    out=Ab[e3][:, c0:c0 + 128], in0=Ab[e3][:, c0:c0 + 128], in1=msb, op=ALU.mult),
                    reads=[("A", e3), "cb"], writes=[("A", e3)])

        def st3(g):
            hl, qt, kb, i, nkb, j, c0, tile = geom(g)
            e3 = g % 4
            ob = 4 + tile % 2
            if i == 0:
                p.op("pe", lambda e, ob=ob, hl=hl, qt=qt: e.matmul(ps[ob][:], lhsT=zer, rhs=QT[:, hl, qt * 512:(qt + 1) * 512],
                                                                 start=True, stop=False),
                     reads=["cb", ("QT", hl, qt)], writes=[PS(ob)])
            p.op("pe", lambda e, kb=kb, c0=c0, e3=e3, ob=ob, hl=hl, last=(i == nkb - 1): e.matmul(
                ps[ob][:, c0:512], lhsT=V[:, kb, hl * 128:(hl + 1) * 128], rhs=Ab[e3][:, c0:512],
                start=False, stop=last),
                reads=[("V", kb // 4), ("A", e3)], writes=[PS(ob)])
            if i == nkb - 1:
                def fin(hl=hl, qt=qt, ob=ob):
                    sl = aoi[0] % 2
                    aoi[0] += 1
                    p.op("dve", lambda e, sl=sl, ob=ob: e.tensor_copy(out=aout[sl][:], in_=ps[ob][:]), reads=[PS(ob)], writes=[("aout", sl)])
                    store_att(1, hl, qt, sl)
                deferred.setdefault(g + 3 + 2, []).append(fin)

        run_pipeline(NB, [st1, st2, st2b, st1b, st3], [0, 1, 2, 0, 3], deferred)

    def dump_att():
        toks = []
        for k in range(2):
            toks.append(p.dma("sp", lambda e, k=k: e.dma_start(out=dbg_att[k * 256:(k + 1) * 256, :], in_=ag2_in[k].ap()),
                              sem=("dbgatt", k)))
        return toks

    def ag2(k):
        p.dma("pool", lambda e, k=k: e.collective_compute("AllGather", ALU.bypass, replica_groups=PAIRS,
                                                          ins=[ag2_in[k].ap().opt()], outs=[ag2_out[k].ap().opt()]),
              writes=[("ag2_out", k)], sem=("cc2", k), inc=1)

    project(0)
    p.dma("pool", lambda e: e.dma_start(out=wqk[:], in_=wqv[:, :, 768:1536]), writes=["wqk"], sem="wqk")
    attn_da()
    p.barrier()
    if stop_after == "da":
        return finish(dump_att())
    ag2(0)
    project(1)
    attn_sb()
    p.barrier()
    if stop_after == "sb":
        return finish(dump_att())
    ag2(1)
    p.barrier()

    areset()
    wgate = take([8, 2048], BF16)
    wout = take([8, D], BF16)
    wab = take([2, 4, D], BF16)
    nts = [take([8, 512], BF16) for i in range(2)]
    ah = [take([8, 512], BF16) for i in range(2)]
    am = take([8, 512], BF16)
    gts = [take([512], F32) for i in range(4)]
    yT = take([8, 512], BF16)
    p.dma("pool", lambda e: e.dma_start(out=wgate, in_=wgate_d.rearrange("(c p) m -> p c m", p=128)), writes=["wgate"], sem="wgate")
    p.dma("pool", lambda e: e.dma_start(out=wab[:, 0, :, :], in_=wa_d.rearrange("(c p) m -> p c m", p=128)), writes=["wa"], sem="wa")
    p.dma("pool", lambda e: e.dma_start(out=wab[:, 1, :, :], in_=wb_d.rearrange("(c p) m -> p c m", p=128)), writes=["wb"], sem="wb")
    p.dma("pool", lambda e: e.dma_start(out=wout, in_=wout_d.rearrange("(c p) m -> p c m", p=128)), writes=["wout"], sem="wout")
    ag1own = [ag1_in[k].ap().rearrange("(c p) t -> p c t", p=128) for k in range(2)]
    ag2v = [ag2_out[k].ap().rearrange("(q p) t -> p q t", p=128) for k in range(2)]
    for tt in range(NTT):
        sl = tt % 2
        p.dma("sp", lambda e, tt=tt, sl=sl: e.dma_start(
            out=nts[sl][:], in_=ag1own[tt // 2][:, :, (tt % 2) * 512:(tt % 2 + 1) * 512]),
            writes=[("nts", sl, 0), ("nts", sl, 1)], sem=("nts", sl))
        for half in range(2):
            for k in range(2):
                p.dma("sp", lambda e, tt=tt, half=half, k=k: e.dma_start(
                    out=ah[half][:, k * 4:(k + 1) * 4, :], in_=ag2v[k][:, :, half * T + tt * 512: half * T + (tt + 1) * 512]),
                    writes=[("ah", half, k)], sem=("ah", half, k))
        p.op("dve", lambda e: e.tensor_scalar(out=am[:], in0=ah[0][:], scalar1=sel[:, 0:1], scalar2=None, op0=ALU.mult),
             reads=[("ah", 0, 0), ("ah", 0, 1), "sel"], writes=["am"])
        p.op("dve", lambda e: e.scalar_tensor_tensor(out=am[:], in0=ah[1][:], scalar=sel[:, 1:2], in1=am[:],
                                                     op0=ALU.mult, op1=ALU.add),
             reads=[("ah", 1, 0), ("ah", 1, 1), "sel", "am"], writes=["am"])
        for mc in range(8):
            par = mc % 2
            for br in range(2):
                bank = br * 2 + par
                col = br * 1024 + mc * 128
                for c in range(8):
                    p.op("pe", lambda e, c=c, col=col, bank=bank, sl=sl: e.matmul(
                        ps[bank][:], lhsT=wgate[:, c, col:col + 128], rhs=nts[sl][:, c, :], start=(c == 0), stop=(c == 7)),
                        reads=["wgate", ("nts", sl, 0), ("nts", sl, 1)], writes=[PS(bank)])
                p.op("act", lambda e, br=br, bank=bank, par=par, mc=mc: e.activation(
                    out=gts[br * 2 + par][:], in_=ps[bank][:], func=AF.Sigmoid, bias=bgate[:, br * 8 + mc: br * 8 + mc + 1]),
                    reads=[PS(bank), "bgate"], writes=[("gt", br, par)])
            for br in range(2):
                bank = 4 + br * 2 + par
                for hh in range(4):
                    k = br * 4 + hh
                    p.op("pe", lambda e, hh=hh, k=k, br=br, bank=bank, mc=mc: e.matmul(
                        ps[bank][:], lhsT=wab[:, br, hh, mc * 128:(mc + 1) * 128], rhs=am[:, k, :], start=(hh == 0), stop=(hh == 3)),
                        reads=["wa" if br == 0 else "wb", "am"], writes=[PS(bank)])
            p.op("dve", lambda e, par=par: e.tensor_tensor(out=gts[par][:], in0=gts[par][:], in1=ps[4 + par][:], op=ALU.mult),
                 reads=[("gt", 0, par), PS(4 + par)], writes=[("gt", 0, par)])
            p.op("dve", lambda e, par=par: e.tensor_tensor(out=gts[2 + par][:], in0=gts[2 + par][:], in1=ps[6 + par][:], op=ALU.mult),
                 reads=[("gt", 1, par), PS(6 + par)], writes=[("gt", 1, par)])
            p.op("dve", lambda e, par=par, mc=mc: e.tensor_tensor(out=yT[:, mc, :], in0=gts[par][:], in1=gts[2 + par][:], op=ALU.add),
                 reads=[("gt", 0, par), ("gt", 1, par)], writes=[("yT", mc)])
        for tb4 in range(4):
            tb = tt * 4 + tb4
            for mh in range(2):
                bank = (tb4 * 2 + mh) % 2
                for mc in range(8):
                    p.op("pe", lambda e, mc=mc, tb4=tb4, mh=mh, bank=bank: e.matmul(
                        ps[bank][:], lhsT=yT[:, mc, tb4 * 128:(tb4 + 1) * 128], rhs=wout[:, mc, mh * 512:(mh + 1) * 512],
                        start=(mc == 0), stop=(mc == 7)), reads=[("yT", mc), "wout"], writes=[PS(bank)])
                p.op("dve", lambda e, tb=tb, mh=mh, bank=bank: e.tensor_tensor(
                    out=h[:, tb, mh * 512:(mh + 1) * 512], in0=ps[bank][:], in1=h[:, tb, mh * 512:(mh + 1) * 512], op=ALU.add),
                    reads=[PS(bank), ("h", tb)], writes=[("h", tb)])
    p.barrier()
    if stop_after == "merge":
        return finish(dump_h("d"))

    fstate = {}
    out_toks = []

    def pre_final(xn_f):
        fstate["obs"] = [take([D], F32) for i in range(5)]
        fstate["xn"] = xn_f
        p.dma("sp", lambda e: e.dma_start(out=gbuf[:], in_=gains_d[3]), writes=["gbuf"], sem="gbuf")
        p.op("dve", lambda e: e.memset(st[:, 0, :], 0.0), writes=["ss"])

    def final_tb(tb):
        obs = fstate["obs"]
        jb = fstate["xn"][tb % 2]
        sl = tb % 5
        p.op("act", lambda e, tb=tb, jb=jb: e.activation(out=jb[:], in_=h[:, tb, :], func=AF.Square,
                                                         accum_out=st[:, 0, tb:tb + 1]),
             reads=[("h", tb), "ss"], writes=[("ss", tb), ("xn", tb % 2)])
        p.op("dve", lambda e, tb=tb: e.tensor_scalar(out=st[:, 1, tb:tb + 1], in0=st[:, 0, tb:tb + 1], scalar1=1.0 / D, scalar2=EPS,
                                                     op0=ALU.mult, op1=ALU.add), reads=[("ss", tb)], writes=[("ms", tb)])
        p.op("act", lambda e, tb=tb: e.activation(out=st[:, 2, tb:tb + 1], in_=st[:, 1, tb:tb + 1], func=AF.Sqrt),
             reads=[("ms", tb)], writes=[("sd", tb)])
        p.op("dve", lambda e, tb=tb: e.reciprocal(out=st[:, 3, tb:tb + 1], in_=st[:, 2, tb:tb + 1]), reads=[("sd", tb)], writes=[("rstd", tb)])
        p.op("dve", lambda e, tb=tb, sl=sl: e.scalar_tensor_tensor(
            out=obs[sl], in0=h[:, tb, :], scalar=st[:, 3, tb:tb + 1], in1=gbuf[:], op0=ALU.mult, op1=ALU.mult),
            reads=[("h", tb), ("rstd", tb), "gbuf"], writes=[("ob", sl)])
        out_toks.append(p.dma("sp", lambda e, tb=tb, sl=sl: e.dma_start(out=out_d[tb * 128:(tb + 1) * 128, :], in_=obs[sl]),
                              reads=[("ob", sl)], sem=("ob", sl)))

    if stop_after == "ffn2":
        ffn(1)
        return finish(dump_h("e"))
    ffn(1, on_final=final_tb, pre_final=pre_final)
    p.wait_all("sp", out_toks)
    p.wait_all("act", out_toks)
    p.emit(nc)
    return nc


def _consts():
    bf = ml_dtypes.bfloat16
    i = np.arange(128)
    cb = np.zeros((128, 6, 128), np.float32)
    cb[:, 0, :] = np.eye(128)
    cb[:, 1, :] = (i[:, None] >= i[None, :])
    cb[:, 2, :] = 1.0
    cb[:, 3, :] = (i[:, None] <= i[None, :])
    cb[:, 4, :] = (i[:, None] < i[None, :])
    return cb.astype(bf)


def make_in_maps(inputs):
    f = lambda a: np.ascontiguousarray(np.asarray(a, dtype=np.float32))
    x = f(inputs["x"])
    gains = np.stack([np.broadcast_to(f(inputs[k]).reshape(-1), (128, D)) for k in
                      ("ffn1_norm", "mix_norm", "ffn2_norm", "final_norm")]).astype(np.float32)
    w_in = f(inputs["w_in"])[0]
    cbf = _consts()
    lamv = np.stack([np.broadcast_to(f(inputs[k])[0], (128, 64)) for k in
                     ("lambda_q1", "lambda_k1", "lambda_q2", "lambda_k2")], axis=1).astype(np.float32)
    subln = f(inputs["diff_subln"])[0].reshape(128, 1)
    bg = f(inputs["b_gate"])[0].reshape(16, 128).T.copy()
    slopes = 2.0 ** (-8.0 * np.arange(1, 5) / 4.0)
    maps = []
    shared = dict(
        gains=gains, wg1=f(inputs["ffn1_w_gate"])[0], wu1=f(inputs["ffn1_w_up"])[0], wd1=f(inputs["ffn1_w_down"])[0],
        wg2=f(inputs["ffn2_w_gate"])[0], wu2=f(inputs["ffn2_w_up"])[0], wd2=f(inputs["ffn2_w_down"])[0],
        wgate=np.ascontiguousarray(w_in[:, 3072:5120]), bgate=bg, wa=f(inputs["w_branch_diff"])[0],
        wb=f(inputs["w_branch_sb"])[0], wout=f(inputs["w_out"])[0], lamv=lamv, subln=subln, cbf=cbf)
    pidx = np.arange(128, dtype=np.float64)[:, None]
    jj = np.arange(-28, 4, dtype=np.float64)[None, :]
    for c in range(8):
        b, r = c // 2, c % 2
        cols = []
        for base in (0, 512, 1024, 1536, 2048, 2560):
            cols.append(w_in[:, base + r * 256: base + (r + 1) * 256])
        wqkv = np.ascontiguousarray(np.concatenate(cols, axis=1))
        ab = np.stack([slopes[2 * r + hl] * (pidx + 128.0 * jj - 256.0) for hl in range(2)], axis=1).astype(np.float32)
        sel = np.zeros((128, 2), np.float32)
        sel[:, r] = 1.0
        m = dict(shared)
        m.update(x=np.ascontiguousarray(x[b, r * T:(r + 1) * T, :]), wqkv=wqkv, abias=ab, sel=sel)
        maps.append(m)
    return maps


_CACHE = {}


def kernel(**inputs):
    if "nc" not in _CACHE:
        _CACHE["nc"] = build_program()
    nc = _CACHE["nc"]
    maps = make_in_maps(inputs)
    res = run_bass_kernel_spmd(nc, maps, core_ids=list(range(8)))
    out = np.empty((4, S, D), np.float32)
    for c in range(8):
        b, r = c // 2, c % 2
        out[b, r * T:(r + 1) * T, :] = res.results[c]["out"]
    return out
```

```python
import os
import numpy as np
import ml_dtypes
import concourse.bass as bass
import concourse.mybir as mybir
from concourse.bass_utils import run_bass_kernel_spmd

F32 = mybir.dt.float32
BF16 = mybir.dt.bfloat16
AF = mybir.ActivationFunctionType
ALU = mybir.AluOpType

ENGS = ("pe", "act", "dve", "pool", "sp")
D = 1024
DFF = 2816
S = 4096
T = 2048
NTB = 16
NTT = 4
PARTS = [(0, 6), (6, 6), (12, 5), (17, 5)]
EPS = 1e-5
LAM_INIT = 0.8 - 0.6 * 1.0
PAIRS = [[0, 1], [2, 3], [4, 5], [6, 7]]
NJ = 32


def _freeze(fn):
    import types
    if fn is None or fn.__closure__ is None:
        return fn
    cells = []
    for c in fn.__closure__:
        try:
            cells.append(types.CellType(c.cell_contents))
        except ValueError:
            cells.append(c)
    return types.FunctionType(fn.__code__, fn.__globals__, fn.__name__, fn.__defaults__, tuple(cells))


class Prog:
    def __init__(self):
        self.ops = []
        self.eng_ops = {e: [] for e in ENGS}
        self.lastw = {}
        self.readers = {}
        self.sem_counts = {}
        self.sem_inc = {}
        self.last_c = {}
        self.dma_since = []

    def _collect(self, reads, writes, tok):
        deps = []
        for r in reads:
            t = self.lastw.get(r)
            if t is not None:
                deps.append(t)
        for w in writes:
            t = self.lastw.get(w)
            if t is not None:
                deps.append(t)
            deps.extend(self.readers.get(w, ()))
        for w in writes:
            self.lastw[w] = tok
            self.readers[w] = []
        for r in reads:
            if r in writes:
                continue
            self.readers.setdefault(r, []).append(tok)
        return [d for d in deps if d != tok]

    def op(self, eng, fn, reads=(), writes=()):
        idx = len(self.eng_ops[eng])
        tok = ("c", eng, idx)
        deps = self._collect(tuple(reads), tuple(writes), tok)
        o = dict(eng=eng, fn=_freeze(fn), deps=deps, kind="c", tok=tok)
        self.eng_ops[eng].append(o)
        self.ops.append(o)
        self.last_c[eng] = tok
        return tok

    def dma(self, eng, fn, reads=(), writes=(), sem=None, inc=16):
        n = self.sem_counts.get(sem, 0) + 1
        self.sem_counts[sem] = n
        self.sem_inc[sem] = inc
        tok = ("d", sem, n)
        deps = self._collect(tuple(reads), tuple(writes), tok)
        o = dict(eng=eng, fn=_freeze(fn), deps=deps, kind="d", tok=tok, sem=sem, inc=inc)
        self.eng_ops[eng].append(o)
        self.ops.append(o)
        self.dma_since.append(tok)
        return tok

    def wait_all(self, eng, toks):
        o = dict(eng=eng, fn=None, deps=list(toks), kind="w", tok=None)
        self.eng_ops[eng].append(o)
        self.ops.append(o)

    def barrier(self, carry=()):
        carry = dict(carry)
        skip = set(carry.values())
        toks = list(self.last_c.values()) + [t for t in self.dma_since if t not in skip]
        self.dma_since = [t for t in self.dma_since if t in skip]
        for e in ENGS:
            self.wait_all(e, toks)
        self.lastw = {}
        self.readers = {}
        for k, t in carry.items():
            self.lastw[k] = t

    def emit(self, nc):
        tokvc = {}
        cur = {e: {} for e in ENGS}
        milestones = set()

        def covered(clock, t):
            if t[0] == "c":
                return clock.get(("c", t[1]), -1) >= t[2]
            return clock.get(("d", t[1]), 0) >= t[2]

        def merge(clock, other):
            for k, v in other.items():
                if clock.get(k, -1) < v:
                    clock[k] = v

        for o in self.ops:
            e = o["eng"]
            clock = cur[e]
            waits = []
            for t in o["deps"]:
                if t[0] == "c" and t[1] == e and e == "pe":
                    continue
                if covered(clock, t):
                    continue
                waits.append(t)
                merge(clock, tokvc[t])
                if t[0] == "c":
                    milestones.add(t)
            best = {}
            for t in waits:
                k = (t[0], t[1])
                if k not in best or best[k][2] < t[2]:
                    best[k] = t
            o["waits"] = list(best.values())
            if o["tok"] is not None:
                know = dict(clock)
                t = o["tok"]
                k = (t[0], t[1])
                know[k] = max(know.get(k, -1), t[2])
                tokvc[t] = know
        msval = {}
        for e in ENGS:
            c = 0
            for o in self.eng_ops[e]:
                if o["kind"] == "c" and o["tok"] in milestones:
                    c += 1
                    msval[o["tok"]] = c
        esem = {e: nc.alloc_semaphore(name=f"es_{e}") for e in ENGS}
        dsem = {k: nc.alloc_semaphore(name=f"ds_{i}") for i, k in enumerate(self.sem_counts)}
        self.n_sems = len(esem) + len(dsem)

        def run(e, eng):
            for o in self.eng_ops[e]:
                for t in o["waits"]:
                    if t[0] == "c":
                        eng.wait_ge(esem[t[1]], msval[t])
                    else:
                        eng.wait_ge(dsem[t[1]], t[2] * self.sem_inc[t[1]])
                if o["fn"] is None:
                    continue
                ins = o["fn"](eng)
                if o["kind"] == "c":
                    if o["tok"] in milestones:
                        ins.then_inc(esem[e], 1)
                else:
                    ins.then_inc(dsem[o["sem"]], o["inc"])

        with nc.Block() as block:
            @block.tensor
            def _(eng):
                run("pe", eng)

            @block.scalar
            def _(eng):
                run("act", eng)

            @block.vector
            def _(eng):
                run("dve", eng)

            @block.gpsimd
            def _(eng):
                run("pool", eng)

            @block.sync
            def _(eng):
                run("sp", eng)


def build_program(stop_after=None, dbg=False, lite=False):
    nc = bass.Bass("TRN2", target_bir_lowering=False)
    p = Prog()

    def din(name, shape, dt=F32):
        if lite and name[0] == "w":
            shape = [128, 128]
        return nc.dram_tensor(name, list(shape), dt, kind="ExternalInput").ap()

    x_d = din("x", [T, D])
    gains_d = din("gains", [4, 128, D])
    wg_d = [din("wg1", [D, DFF]), din("wg2", [D, DFF])]
    wu_d = [din("wu1", [D, DFF]), din("wu2", [D, DFF])]
    wd_d = [din("wd1", [DFF, D]), din("wd2", [DFF, D])]
    wqkv_d = din("wqkv", [D, 1536])
    wgate_d = din("wgate", [D, 2048])
    bgate_d = din("bgate", [128, 16])
    wa_d = din("wa", [512, D])
    wb_d = din("wb", [512, D])
    wout_d = din("wout", [D, D])
    lamv_d = din("lamv", [128, 4, 64])
    subln_d = din("subln", [128, 1])
    abias_d = din("abias", [128, 2, NJ])
    sel_d = din("sel", [128, 2])
    cb_d = din("cbf", [128, 6, 128], BF16)
    out_d = nc.dram_tensor("out", [T, D], F32, kind="ExternalOutput").ap()
    if dbg:
        dbg_h = nc.dram_tensor("dbg_h", [T, D], F32, kind="ExternalOutput").ap()
        dbg_att = nc.dram_tensor("dbg_att", [512, S], BF16, kind="ExternalOutput").ap()
    ag1_in = [nc.dram_tensor(f"ag1_in{k}", [D, T // 2], BF16) for k in range(2)]
    ag1_out = [nc.dram_tensor(f"ag1_out{k}", [2 * D, T // 2], BF16) for k in range(2)]
    ag2_in = [nc.dram_tensor(f"ag2_in{k}", [256, S], BF16) for k in range(2)]
    ag2_out = [nc.dram_tensor(f"ag2_out{k}", [512, S], BF16) for k in range(2)]

    h = nc.alloc_sbuf_tensor("h", [128, NTB, D], F32)
    cb = nc.alloc_sbuf_tensor("cb", [128, 6, 128], BF16)
    ident, tri, ones, mda, msb, zer = (cb[:, i, :] for i in range(6))
    gbuf = nc.alloc_sbuf_tensor("gbuf", [128, D], F32)
    st = nc.alloc_sbuf_tensor("st", [128, 4, NTB], F32)
    small = nc.alloc_sbuf_tensor("small", [128, 16], F32)
    ARENA = 66560
    arena = nc.alloc_sbuf_tensor("arena", [128, ARENA], BF16)
    aoff = [0]

    def areset():
        aoff[0] = 0

    def take(shape, dt):
        n = 1
        for d_ in shape:
            n *= d_
        ne = n * (2 if dt == F32 else 1)
        ne = (ne + 15) // 16 * 16
        assert aoff[0] + ne <= ARENA, (aoff[0], ne)
        v = arena[:, aoff[0]:aoff[0] + ne]
        aoff[0] += ne
        if dt == F32:
            v = v.bitcast(F32)
        v = v[:, 0:n]
        if len(shape) == 2:
            v = v.rearrange("p (a b) -> p a b", a=shape[0])
        elif len(shape) == 3:
            v = v.rearrange("p (a b c) -> p a b c", a=shape[0], b=shape[1])
        return v

    pp = [nc.alloc_psum_tensor(f"pp{i}", [128, 2, 512], F32) for i in range(4)]
    ps = [pp[i // 2][:, i % 2, :] for i in range(8)]
    psb = [t.bitcast(BF16) for t in ps]

    def PS(i):
        return ("ps", i)

    tx = []
    for q in range(4):
        tx.append(p.dma("sp", lambda e, q=q: e.dma_start(
            out=h[:, q * 4:(q + 1) * 4, :],
            in_=x_d[q * 512:(q + 1) * 512, :].rearrange("(tb p) d -> p tb d", p=128)),
            writes=[("h", q * 4 + i) for i in range(4)], sem=("hload", q)))
    p.dma("sp", lambda e: e.dma_start(out=cb[:], in_=cb_d), writes=["cb"], sem="cb")

    def norm_to_T(gi, dstT, xn, junk, on_tb=None):
        p.dma("sp", lambda e: e.dma_start(out=gbuf[:], in_=gains_d[gi]), writes=["gbuf"], sem="gbuf")
        p.op("dve", lambda e: e.memset(st[:, 0, :], 0.0), writes=["ss"])
        for tb in range(NTB):
            jb = xn[tb % 2]
            p.op("act", lambda e, tb=tb, jb=jb: e.activation(out=jb[:], in_=h[:, tb, :], func=AF.Square,
                                                             accum_out=st[:, 0, tb:tb + 1]),
                 reads=[("h", tb), "ss"], writes=[("ss", tb), ("xn", tb % 2)])
        p.op("dve", lambda e: e.tensor_scalar(out=st[:, 1, :], in0=st[:, 0, :], scalar1=1.0 / D, scalar2=EPS,
                                              op0=ALU.mult, op1=ALU.add), reads=[("ss", i) for i in range(NTB)], writes=["ms"])
        p.op("act", lambda e: e.activation(out=st[:, 2, :], in_=st[:, 1, :], func=AF.Sqrt), reads=["ms"], writes=["sd"])
        p.op("dve", lambda e: e.reciprocal(out=st[:, 3, :], in_=st[:, 2, :]), reads=["sd"], writes=["rstd"])
        import os
        NTBX = int(os.environ.get("NTBX", NTB))
        for tb in range(NTBX):
            sl = tb % 2
            if os.environ.get("NOSTT"):
                p.op("dve", lambda e, tb=tb, sl=sl: e.tensor_copy(out=xn[sl][:], in_=h[:, tb, :]),
                     reads=[("h", tb), "rstd", "gbuf"], writes=[("xn", sl)])
            else:
                p.op("dve", lambda e, tb=tb, sl=sl: e.scalar_tensor_tensor(
                    out=xn[sl][:], in0=h[:, tb, :], scalar=st[:, 3, tb:tb + 1], in1=gbuf[:],
                    op0=ALU.mult, op1=ALU.mult), reads=[("h", tb), "rstd", "gbuf"], writes=[("xn", sl)])
            bA, bB = (6, 7) if tb % 2 == 0 else (4, 5)
            for c in range(8):
                bank = bA if c < 4 else bB
                p.op("pe", lambda e, c=c, sl=sl, bank=bank: e.transpose(
                    out=psb[bank][:, (c % 4) * 128:(c % 4 + 1) * 128], in_=xn[sl][:, c * 128:(c + 1) * 128], identity=ident),
                    reads=[("xn", sl), "cb"], writes=[PS(bank)])
            srcA = psb[bA][:, 0:512].rearrange("p (c t) -> p c t", c=4)
            srcB = psb[bB][:, 0:512].rearrange("p (c t) -> p c t", c=4)
            p.op("act", lambda e, tb=tb, srcA=srcA: e.copy(out=dstT[:, 0:4, tb * 128:(tb + 1) * 128], in_=srcA),
                 reads=[PS(bA)], writes=[("xT", tb // 4, 0)])
            p.op("dve", lambda e, tb=tb, srcB=srcB: e.tensor_copy(out=dstT[:, 4:8, tb * 128:(tb + 1) * 128], in_=srcB),
                 reads=[PS(bB)], writes=[("xT", tb // 4, 1)])
            if on_tb is not None:
                on_tb(tb)

    def xT_keys(tt):
        return [("xT", tt, 0), ("xT", tt, 1)]

    def ffn(fi, on_final=None, pre_final=None):
        areset()
        xnT = take([8, T], BF16)
        actT = take([6, T], BF16)
        wdb = [take([6, D], BF16) for i in range(2)]
        wgb = [take([8, 256], BF16) for i in range(2)]
        wub = [take([8, 256], BF16) for i in range(2)]
        xn = [take([D], BF16) for i in range(2)]
        sg = [take([512], F32) for i in range(2)]
        junk = take([D], BF16)
        norm_to_T(0 if fi == 0 else 2, xnT, xn, junk)
        if stop_after == "norm1":
            p.barrier()
            return
        wgv = wg_d[fi].rearrange("(c p) f -> p c f", p=128)
        wuv = wu_d[fi].rearrange("(c p) f -> p c f", p=128)
        wdv = wd_d[fi].rearrange("(c p) m -> p c m", p=128)
        gi = 0
        it = 0
        dn = 0
        for pi, (fc0, nfc) in enumerate(PARTS):
            if stop_after == "part1" and pi >= 1:
                break
            wsl = pi % 2
            p.dma("pool", lambda e, wsl=wsl, fc0=fc0, nfc=nfc: e.dma_start(
                out=wdb[wsl][:, 0:nfc, :], in_=wdv[:, fc0:fc0 + nfc, :]), writes=[("wd", wsl)], sem=("wd", fi, wsl))
            groups = []
            k = 0
            while k < nfc:
                n = min(2, nfc - k)
                groups.append((k, n))
                k += n
            for (k0, n) in groups:
                gs = gi % 2
                gi += 1
                col0 = (fc0 + k0) * 128
                p.dma("pool", lambda e, gs=gs, col0=col0, n=n: e.dma_start(
                    out=wgb[gs][:, :, 0:n * 128], in_=wgv[:, :, col0:col0 + n * 128]), writes=[("wg", gs)], sem=("wg", fi, gs))
                p.dma("pool", lambda e, gs=gs, col0=col0, n=n: e.dma_start(
                    out=wub[gs][:, :, 0:n * 128], in_=wuv[:, :, col0:col0 + n * 128]), writes=[("wu", gs)], sem=("wu", fi, gs))
                for jj in range(n):
                    fcl = k0 + jj
                    for tt in range(NTT):
                        b = it % 2
                        it += 1
                        for c in range(8):
                            p.op("pe", lambda e, gs=gs, jj=jj, c=c, tt=tt, b=b: e.matmul(
                                ps[b][:], lhsT=wgb[gs][:, c, jj * 128:(jj + 1) * 128], rhs=xnT[:, c, tt * 512:(tt + 1) * 512],
                                start=(c == 0), stop=(c == 7)), reads=[("wg", gs)] + xT_keys(tt), writes=[PS(b)])
                        for c in range(8):
                            p.op("pe", lambda e, gs=gs, jj=jj, c=c, tt=tt, b=b: e.matmul(
                                ps[2 + b][:], lhsT=wub[gs][:, c, jj * 128:(jj + 1) * 128], rhs=xnT[:, c, tt * 512:(tt + 1) * 512],
                                start=(c == 0), stop=(c == 7)), reads=[("wu", gs)] + xT_keys(tt), writes=[PS(2 + b)])
                        p.op("act", lambda e, b=b: e.activation(out=sg[b][:], in_=ps[b][:], func=AF.Silu),
                             reads=[PS(b)], writes=[("sg", b)])
                        p.op("dve", lambda e, b=b, fcl=fcl, tt=tt: e.tensor_tensor(
                            out=actT[:, fcl, tt * 512:(tt + 1) * 512], in0=sg[b][:], in1=ps[2 + b][:], op=ALU.mult),
                            reads=[("sg", b), PS(2 + b)], writes=[("actT", fcl, tt)])
            lastp = (pi == len(PARTS) - 1)
            if lastp and pre_final is not None:
                pre_final(xn)
            for tb in range(NTB):
                for mh in range(2):
                    b = 4 + dn % 2
                    dn += 1
                    for fcl in range(nfc):
                        p.op("pe", lambda e, fcl=fcl, tb=tb, mh=mh, b=b, wsl=wsl, nfc=nfc: e.matmul(
                            ps[b][:], lhsT=actT[:, fcl, tb * 128:(tb + 1) * 128], rhs=wdb[wsl][:, fcl, mh * 512:(mh + 1) * 512],
                            start=(fcl == 0), stop=(fcl == nfc - 1)),
                            reads=[("actT", fcl, tb // 4), ("wd", wsl)], writes=[PS(b)])
                    p.op("dve", lambda e, tb=tb, mh=mh, b=b: e.scalar_tensor_tensor(
                        out=h[:, tb, mh * 512:(mh + 1) * 512], in0=ps[b][:], scalar=0.5, in1=h[:, tb, mh * 512:(mh + 1) * 512],
                        op0=ALU.mult, op1=ALU.add), reads=[PS(b), ("h", tb)], writes=[("h", tb)])
                if lastp and on_final is not None:
                    on_final(tb)
        if fi == 0 or stop_after == "ffn2":
            p.barrier()
        return xn

    def dump_h(tag):
        toks = []
        for q in range(4):
            toks.append(p.dma("sp", lambda e, q=q: e.dma_start(
                out=dbg_h[q * 512:(q + 1) * 512, :].rearrange("(tb p) d -> p tb d", p=128), in_=h[:, q * 4:(q + 1) * 4, :]),
                reads=[("h", q * 4 + i) for i in range(4)], sem=("dbgh", tag, q)))
        return toks

    def finish(extra):
        p.wait_all("sp", extra)
        p.emit(nc)
        return nc

    if stop_after == "load":
        return finish(dump_h("l"))
    if not os.environ.get("SKIPFFN"):
        ffn(0)
    if stop_after in ("ffn1", "norm1", "part1"):
        return finish(dump_h("a"))

    areset()
    wqk = take([8, 768], BF16)
    wqv = wqkv_d.rearrange("(c p) f -> p c f", p=128)
    tok_wqk = p.dma("pool", lambda e: e.dma_start(out=wqk, in_=wqv[:, :, 0:768]), writes=["wqk"], sem="wqk")
    nT = take([8, T], BF16)
    xn0 = [take([D], BF16) for i in range(2)]
    junk0 = take([D], BF16)
    cc1_tok = {}

    def after_tb(tb):
        if tb % 8 != 7:
            return
        k = tb // 8
        for c in range(8):
            p.dma("sp", lambda e, c=c, k=k: e.dma_start(out=ag1_in[k].ap()[c * 128:(c + 1) * 128, :],
                                                       in_=nT[:, c, k * 1024:(k + 1) * 1024]),
                  reads=xT_keys(2 * k) + xT_keys(2 * k + 1), writes=[("ag1_in", k, c)], sem=("ag1w", k, c))
        cc1_tok[k] = p.dma("pool", lambda e, k=k: e.collective_compute("AllGather", ALU.bypass, replica_groups=PAIRS,
                                                                     ins=[ag1_in[k].ap().opt()], outs=[ag1_out[k].ap().opt()]),
                           reads=[("ag1_in", k, c) for c in range(8)], writes=[("ag1_out", k)], sem=("cc1", k), inc=1)

    norm_to_T(1, nT, xn0, junk0, on_tb=after_tb)
    lamv = nc.alloc_sbuf_tensor("lamv_sb", [128, 4, 64], F32)
    lprod = nc.alloc_sbuf_tensor("lprod_sb", [128, 2, 64], F32)
    subl = nc.alloc_sbuf_tensor("subl_sb", [128, 2], F32)
    abias = nc.alloc_sbuf_tensor("abias_sb", [128, 2, NJ], F32)
    sel = nc.alloc_sbuf_tensor("sel_sb", [128, 2], F32)
    bgate = nc.alloc_sbuf_tensor("bgate_sb", [128, 16], F32)
    p.dma("sp", lambda e: e.dma_start(out=lamv[:], in_=lamv_d), writes=["lamv"], sem="lamv")
    p.dma("sp", lambda e: e.dma_start(out=subl[:, 0:1], in_=subln_d), writes=["subl0"], sem="subl")
    p.dma("sp", lambda e: e.dma_start(out=abias[:], in_=abias_d), writes=["abias"], sem="abias")
    p.dma("sp", lambda e: e.dma_start(out=sel[:], in_=sel_d), writes=["sel"], sem="sel")
    p.dma("sp", lambda e: e.dma_start(out=bgate[:], in_=bgate_d), writes=["bgate"], sem="bgate")
    p.op("dve", lambda e: e.tensor_tensor(out=lprod[:, 0, :], in0=lamv[:, 0, :], in1=lamv[:, 1, :], op=ALU.mult),
         reads=["lamv"], writes=["lprod0"])
    p.op("dve", lambda e: e.tensor_tensor(out=lprod[:, 1, :], in0=lamv[:, 2, :], in1=lamv[:, 3, :], op=ALU.mult),
         reads=["lamv"], writes=["lprod1"])
    p.op("dve", lambda e: e.reduce_sum(out=small[:, 0:2], in_=lprod[:], axis=mybir.AxisListType.X),
         reads=["lprod0", "lprod1"], writes=["sm01"])
    p.op("act", lambda e: e.activation(out=small[:, 2:4], in_=small[:, 0:2], func=AF.Exp), reads=["sm01"], writes=["sm23"])
    p.op("dve", lambda e: e.tensor_tensor(out=small[:, 4:5], in0=small[:, 3:4], in1=small[:, 2:3], op=ALU.subtract),
         reads=["sm23"], writes=["sm4"])
    p.op("dve", lambda e: e.tensor_scalar(out=small[:, 5:6], in0=small[:, 4:5], scalar1=-LAM_INIT, scalar2=0.0,
                                          op0=ALU.add, op1=ALU.add), reads=["sm4"], writes=["neglam"])
    p.op("dve", lambda e: e.tensor_scalar(out=subl[:, 1:2], in0=subl[:, 0:1], scalar1=1.0 - LAM_INIT, scalar2=0.0,
                                          op0=ALU.mult, op1=ALU.add), reads=["subl0"], writes=["subl1"])
    neglam = small[:, 5:6]
    carry = {("ag1_out", k): cc1_tok[k] for k in range(2)}
    carry["wqk"] = tok_wqk
    p.barrier(carry=carry)
    if stop_after == "ag1":
        return finish(dump_h("b"))

    areset()
    wqk = take([8, 768], BF16)
    nts = [take([8, 512], BF16) for i in range(2)]
    QT = take([2, S], BF16)
    KT = take([2, S], BF16)
    V = take([32, 256], BF16)
    Pp = [take([2, 512], BF16) for i in range(3)]
    Pm = [[Pp[i][:, m, :] for i in range(3)] for m in range(2)]
    fin = [take([512], F32) for i in range(4)]
    fin2 = [take([512], F32) for i in range(4)]
    sqb = take([512], BF16)
    sqb2 = take([512], BF16)
    acc0 = [take([512], F32) for i in range(2)]
    accb = [take([512], BF16) for i in range(2)]
    aout = [take([512], BF16) for i in range(2)]
    Eb = [take([512], F32) for i in range(4)]
    Ub = [take([512], BF16) for i in range(2)]
    Xb = [take([512], F32) for i in range(2)]
    Ab = [take([512], BF16) for i in range(4)]
    Usum2 = [take([512], BF16) for i in range(2)]
    ag1v = [ag1_out[k].ap().rearrange("(r c p) t -> p r c t", r=2, c=8, p=128) for k in range(2)]
    aoi = [0]

    def project(br):
        qscale = 1.0 if br == 0 else 128.0 ** -0.5
        for ti, t8 in enumerate([0, 1, 4, 5, 2, 3, 6, 7]):
            r, tl = t8 // 4, t8 % 4
            sl = ti % 2
            piece, tl2 = tl // 2, tl % 2
            p.dma("sp", lambda e, r=r, tl2=tl2, sl=sl, piece=piece: e.dma_start(
                out=nts[sl][:], in_=ag1v[piece][:, r, :, tl2 * 512:(tl2 + 1) * 512]),
                reads=[("ag1_out", piece)], writes=[("nts", sl, 0), ("nts", sl, 1)], sem=("nts", sl))
            cnt = 0
            for which in range(2):
                for hl in range(2):
                    b = cnt % 2
                    cnt += 1
                    col = which * 256 + hl * 128
                    for c in range(8):
                        p.op("pe", lambda e, c=c, sl=sl, col=col, b=b: e.matmul(
                            ps[b][:], lhsT=wqk[:, c, col:col + 128], rhs=nts[sl][:, c, :], start=(c == 0), stop=(c == 7)),
                            reads=["wqk", ("nts", sl, 0), ("nts", sl, 1)], writes=[PS(b)])
                    if which == 0:
                        p.op("act", lambda e, hl=hl, t8=t8, b=b: e.mul(out=QT[:, hl, t8 * 512:(t8 + 1) * 512], in_=ps[b][:], mul=qscale),
                             reads=[PS(b)], writes=[("QT", hl, t8)])
                    else:
                        p.op("dve", lambda e, hl=hl, t8=t8, b=b: e.tensor_copy(out=KT[:, hl, t8 * 512:(t8 + 1) * 512], in_=ps[b][:]),
                             reads=[PS(b)], writes=[("KT", hl, t8)])
            for tb4 in range(4):
                b = 2 + tb4 % 2
                for c in range(8):
                    p.op("pe", lambda e, c=c, sl=sl, tb4=tb4, b=b: e.matmul(
                        ps[b][:, 0:256], lhsT=nts[sl][:, c, tb4 * 128:(tb4 + 1) * 128], rhs=wqk[:, c, 512:768],
                        start=(c == 0), stop=(c == 7)), reads=["wqk", ("nts", sl, 0), ("nts", sl, 1)], writes=[PS(b)])
                if tb4 % 2 == 0:
                    p.op("act", lambda e, t8=t8, tb4=tb4, b=b: e.copy(out=V[:, t8 * 4 + tb4, :], in_=ps[b][:, 0:256]),
                         reads=[PS(b)], writes=[("V", t8)])
                else:
                    p.op("dve", lambda e, t8=t8, tb4=tb4, b=b: e.tensor_copy(out=V[:, t8 * 4 + tb4, :], in_=ps[b][:, 0:256]),
                         reads=[PS(b)], writes=[("V", t8)])

    def store_att(br, hl, qt, src_sl):
        row0 = hl * 128
        p.dma("sp", lambda e: e.dma_start(out=ag2_in[br].ap()[row0:row0 + 128, qt * 512:(qt + 1) * 512], in_=aout[src_sl][:]),
              reads=[("aout", src_sl)], writes=[("ag2_in", br, hl, qt)], sem=("aout", src_sl))

    def run_pipeline(nblocks, stages, skews, deferred):
        last = nblocks + max(skews)
        it = 0
        while it < last or any(k >= it for k in deferred):
            for st_fn, sk in zip(stages, skews):
                g = it - sk
                if 0 <= g < nblocks:
                    st_fn(g)
            for fn in deferred.pop(it, []):
                fn()
            it += 1
            if it > last + 64:
                break
        for k in sorted(deferred):
            for fn in deferred[k]:
                fn()
        deferred.clear()

    def attn_da():
        blocks = [(hl, qt, kb) for hl in range(2) for qt in range(8) for kb in range(4 * qt + 4)]
        NB = len(blocks)
        deferred = {}
        fsets = [fin, fin2]
        sqbs = [sqb, sqb2]

        def geom(g):
            hl, qt, kb = blocks[g]
            j = kb - 4 * qt
            return hl, qt, kb, j, (128 * j if j > 0 else 0)

        def stage_scores(g):
            hl, qt, kb, j, c0 = geom(g)
            par = g % 2
            p3 = g % 3
            for m in range(2):
                bank = par * 2 + m
                p.op("pe", lambda e, m=m, kb=kb, c0=c0, bank=bank, hl=hl, qt=qt: e.matmul(
                    ps[bank][:, c0:512], lhsT=KT[64 * m:64 * m + 64, hl, kb * 128:(kb + 1) * 128],
                    rhs=QT[64 * m:64 * m + 64, hl, qt * 512 + c0:(qt + 1) * 512], start=True, stop=True),
                    reads=[("KT", hl, kb // 4), ("QT", hl, qt)], writes=[PS(bank)])
            p.op("act", lambda e, c0=c0, par=par, j=j, hl=hl, p3=p3: e.activation(
                out=Pp[p3][:, :, c0:512], in_=pp[par][:, :, c0:512], func=AF.Exp,
                bias=abias[:, hl, j + 28:j + 29], scale=0.125),
                reads=[PS(par * 2), PS(par * 2 + 1), "abias"], writes=[("P", 0, p3), ("P", 1, p3)])
            if j >= 0:
                for m in range(2):
                    p.op("dve", lambda e, m=m, c0=128 * j, p3=p3: e.tensor_tensor(
                        out=Pm[m][p3][:, c0:c0 + 128], in0=Pm[m][p3][:, c0:c0 + 128], in1=mda, op=ALU.mult),
                        reads=[("P", m, p3), "cb"], writes=[("P", m, p3)])

        def stage_acc(g):
            hl, qt, kb, j, c0 = geom(g)
            par = g % 2
            p3 = g % 3
            nkb = 4 * qt + 4
            for m in range(2):
                p.op("pe", lambda e, m=m, kb=kb, c0=c0, p3=p3, hl=hl, nkb=nkb: e.matmul(
                    ps[4 + m][:, c0:512], lhsT=V[:, kb, hl * 128:(hl + 1) * 128], rhs=Pm[m][p3][:, c0:512],
                    start=(kb == 0), stop=(kb == nkb - 1)),
                    reads=[("V", kb // 4), ("P", m, p3)], writes=[PS(4 + m)])
                if m == 1:
                    p.op("pe", lambda e, m=m, c0=c0, p3=p3, kb=kb, nkb=nkb: e.matmul(
                        ps[6 + m][:, c0:512], lhsT=ones, rhs=Pm[m][p3][:, c0:512],
                        start=(kb == 0), stop=(kb == nkb - 1)),
                        reads=["cb", ("P", m, p3)], writes=[PS(6 + m)])
                else:
                    tpa = (hl * 8 + qt) % 2
                    if kb == 0:
                        p.op("dve", lambda e, p3=p3, tpa=tpa: e.tensor_copy(out=acc0[tpa][:], in_=Pm[0][p3][:]),
                             reads=[("P", 0, p3)], writes=[("acc0", tpa)])
                    else:
                        p.op("dve", lambda e, p3=p3, tpa=tpa, c0=c0: e.tensor_tensor(
                            out=acc0[tpa][:, c0:512], in0=acc0[tpa][:, c0:512], in1=Pm[0][p3][:, c0:512], op=ALU.add),
                            reads=[("P", 0, p3), ("acc0", tpa)], writes=[("acc0", tpa)])
            if kb == nkb - 1:
                tp = (hl * 8 + qt) % 2
                l0s, l1s, o0s, o1s = fsets[tp]
                sq = sqbs[tp]
                K = lambda n: (n, tp)
                p.op("dve", lambda e, tp=tp: e.tensor_copy(out=accb[tp][:], in_=acc0[tp][:]), reads=[("acc0", tp)], writes=[("accb", tp)])
                p.op("pe", lambda e, tp=tp: e.matmul(ps[6][:], lhsT=ones, rhs=accb[tp][:], start=True, stop=True),
                     reads=["cb", ("accb", tp)], writes=[PS(6)])
                p.op("act", lambda e, l1s=l1s: e.copy(out=l1s[:], in_=ps[7][:]), reads=[PS(7)], writes=[K("l1s")])
                p.op("dve", lambda e, o0s=o0s: e.tensor_copy(out=o0s[:], in_=ps[4][:]), reads=[PS(4)], writes=[K("o0s")])
                p.op("dve", lambda e, o1s=o1s: e.tensor_copy(out=o1s[:], in_=ps[5][:]), reads=[PS(5)], writes=[K("o1s")])
                p.op("act", lambda e, l0s=l0s: e.copy(out=l0s[:], in_=ps[6][:]), reads=[PS(6)], writes=[K("l0s")])
                it_now = g + 1
                steps = [
                    lambda: p.op("dve", lambda e: e.reciprocal(out=l0s[:], in_=l0s[:]), reads=[K("l0s")], writes=[K("l0s")]),
                    lambda: p.op("dve", lambda e: e.reciprocal(out=l1s[:], in_=l1s[:]), reads=[K("l1s")], writes=[K("l1s")]),
                    lambda: p.op("dve", lambda e: e.tensor_tensor(out=o0s[:], in0=o0s[:], in1=l0s[:], op=ALU.mult),
                                 reads=[K("o0s"), K("l0s")], writes=[K("o0s")]),
                    lambda: p.op("dve", lambda e: e.tensor_tensor(out=o1s[:], in0=o1s[:], in1=l1s[:], op=ALU.mult),
                                 reads=[K("o1s"), K("l1s")], writes=[K("o1s")]),
                    lambda: p.op("dve", lambda e: e.scalar_tensor_tensor(out=o0s[:], in0=o1s[:], scalar=neglam, in1=o0s[:],
                                                                         op0=ALU.mult, op1=ALU.add),
                                 reads=[K("o1s"), K("o0s"), "neglam"], writes=[K("o0s")]),
                    lambda: p.op("act", lambda e: e.activation(out=sq[:], in_=o0s[:], func=AF.Square), reads=[K("o0s")], writes=[K("sq")]),
                    lambda: (p.op("pe", lambda e: e.matmul(ps[6][:], lhsT=ones, rhs=sq[:], start=True, stop=True),
                                  reads=["cb", K("sq")], writes=[PS(6)]),
                             p.op("dve", lambda e: e.tensor_scalar(out=l0s[:], in0=ps[6][:], scalar1=1.0 / 128, scalar2=EPS,
                                                                   op0=ALU.mult, op1=ALU.add), reads=[PS(6)], writes=[K("l0s")])),
                    lambda: p.op("act", lambda e: e.activation(out=l1s[:], in_=l0s[:], func=AF.Ln), reads=[K("l0s")], writes=[K("l1s")]),
                    lambda: p.op("act", lambda e: e.activation(out=l0s[:], in_=l1s[:], func=AF.Exp, scale=-0.5), reads=[K("l1s")], writes=[K("l0s")]),
                ]

                def last_step(hl=hl, qt=qt, o0s=o0s, l0s=l0s, K=K):
                    sl = aoi[0] % 2
                    aoi[0] += 1
                    p.op("dve", lambda e, sl=sl: e.scalar_tensor_tensor(out=aout[sl][:], in0=o0s[:], scalar=subl[:, 1:2], in1=l0s[:],
                                                                        op0=ALU.mult, op1=ALU.mult),
                         reads=[K("o0s"), K("l0s"), "subl1"], writes=[("aout", sl)])
                    store_att(0, hl, qt, sl)
                steps.append(last_step)
                for si, fn in enumerate(steps):
                    deferred.setdefault(it_now + 1 + si, []).append(fn)

        run_pipeline(NB, [stage_scores, stage_acc], [0, 1], deferred)

    def attn_sb():
        blocks = []
        for hl in range(2):
            for qt in range(8):
                nkb = 4 * qt + 4
                for i, kb in enumerate(range(nkb - 1, -1, -1)):
                    blocks.append((hl, qt, kb, i, nkb))
        NB = len(blocks)
        deferred = {}

        def geom(g):
            hl, qt, kb, i, nkb = blocks[g]
            j = kb - 4 * qt
            tile = hl * 8 + qt
            return hl, qt, kb, i, nkb, j, (128 * j if j > 0 else 0), tile

        def st1(g):
            hl, qt, kb, i, nkb, j, c0, tile = geom(g)
            par = g % 2
            e3 = g % 4
            p.op("pe", lambda e, kb=kb, c0=c0, par=par, hl=hl, qt=qt: e.matmul(
                ps[par][:, c0:512], lhsT=KT[:, hl, kb * 128:(kb + 1) * 128],
                rhs=QT[:, hl, qt * 512 + c0:(qt + 1) * 512], start=True, stop=True),
                reads=[("KT", hl, kb // 4), ("QT", hl, qt)], writes=[PS(par)])
            p.op("act", lambda e, c0=c0, par=par, e3=e3: e.activation(out=Eb[e3][:, c0:512], in_=ps[par][:, c0:512], func=AF.Exp),
                 reads=[PS(par)], writes=[("E", e3)])

        def st1b(g):
            hl, qt, kb, i, nkb, j, c0, tile = geom(g)
            par = g % 2
            e3 = g % 4
            p.op("act", lambda e, c0=c0, par=par, e3=e3: e.activation(out=Ub[par][:, c0:512], in_=Eb[e3][:, c0:512], func=AF.Ln, bias=1.0),
                 reads=[("E", e3)], writes=[("U", par)])
            if j >= 0:
                p.op("dve", lambda e, c0=128 * j, par=par: e.tensor_tensor(
                    out=Ub[par][:, c0:c0 + 128], in0=Ub[par][:, c0:c0 + 128], in1=msb, op=ALU.mult),
                    reads=[("U", par), "cb"], writes=[("U", par)])

        def st2(g):
            hl, qt, kb, i, nkb, j, c0, tile = geom(g)
            par = g % 2
            e3 = g % 4
            us_r = Usum2[i % 2]
            us_w = Usum2[(i + 1) % 2]
            kr = ("usum", i % 2)
            kw = ("usum", (i + 1) % 2)
            first = (i == 0)
            p.op("pe", lambda e, c0=c0, par=par, first=first: e.matmul(
                ps[2 + par][:, c0:512], lhsT=tri, rhs=Ub[par][:, c0:512], start=True, stop=first),
                reads=["cb", ("U", par)], writes=[PS(2 + par)])
            if not first:
                p.op("pe", lambda e, c0=c0, par=par, us_r=us_r: e.matmul(
                    ps[2 + par][:, c0:512], lhsT=ones, rhs=us_r[:, c0:512], start=False, stop=True),
                    reads=["cb", kr], writes=[PS(2 + par)])
            if i < nkb - 1:
                if first:
                    p.op("pool", lambda e, us_w=us_w: e.memset(us_w[:], 0.0), writes=[kw])
                    p.op("pool", lambda e, c0=c0, par=par, us_w=us_w: e.tensor_copy(out=us_w[:, c0:512], in_=Ub[par][:, c0:512]),
                         reads=[("U", par)], writes=[kw])
                else:
                    if c0 > 0:
                        p.op("pool", lambda e, c0=c0, us_w=us_w, us_r=us_r: e.tensor_copy(out=us_w[:, 0:c0], in_=us_r[:, 0:c0]),
                             reads=[kr], writes=[kw])
                    p.op("pool", lambda e, c0=c0, par=par, us_w=us_w, us_r=us_r: e.tensor_tensor(
                        out=us_w[:, c0:512], in0=us_r[:, c0:512], in1=Ub[par][:, c0:512], op=ALU.add),
                        reads=[kr, ("U", par)], writes=[kw])

        def st2b(g):
            hl, qt, kb, i, nkb, j, c0, tile = geom(g)
            par = g % 2
            e3 = g % 4
            p.op("act", lambda e, c0=c0, par=par: e.activation(out=Xb[par][:, c0:512], in_=ps[2 + par][:, c0:512],
                                                             func=AF.Exp, scale=-1.0),
                 reads=[PS(2 + par)], writes=[("X", par)])
            p.op("dve", lambda e, c0=c0, par=par, e3=e3: e.tensor_tensor(
                out=Ab[e3][:, c0:512], in0=Eb[e3][:, c0:512], in1=Xb[par][:, c0:512], op=ALU.mult),
                reads=[("E", e3), ("X", par)], writes=[("A", e3)])
            if j >= 0:
                p.op("dve", lambda e, c0=128 * j, e3=e3: e.tensor_tensor(
                    out=Ab[e3][:, c0:c0 + 128], in0=Ab[e3][:, c0:c0 + 128], in1=msb, op=ALU.mult),
                    reads=[("A", e3), "cb"], writes=[("A", e3)])

        def st3(g):
            hl, qt, kb, i, nkb, j, c0, tile = geom(g)
            e3 = g % 4
            ob = 4 + tile % 2
            if i == 0:
                p.op("pe", lambda e, ob=ob, hl=hl, qt=qt: e.matmul(ps[ob][:], lhsT=zer, rhs=QT[:, hl, qt * 512:(qt + 1) * 512],
                                                                 start=True, stop=False),
                     reads=["cb", ("QT", hl, qt)], writes=[PS(ob)])
            p.op("pe", lambda e, kb=kb, c0=c0, e3=e3, ob=ob, hl=hl, last=(i == nkb - 1): e.matmul(
                ps[ob][:, c0:512], lhsT=V[:, kb, hl * 128:(hl + 1) * 128], rhs=Ab[e3][:, c0:512],
                start=False, stop=last),
                reads=[("V", kb // 4), ("A", e3)], writes=[PS(ob)])
            if i == nkb - 1:
                def fin(hl=hl, qt=qt, ob=ob):
                    sl = aoi[0] % 2
                    aoi[0] += 1
                    p.op("dve", lambda e, sl=sl, ob=ob: e.tensor_copy(out=aout[sl][:], in_=ps[ob][:]), reads=[PS(ob)], writes=[("aout", sl)])
                    store_att(1, hl, qt, sl)
                deferred.setdefault(g + 3 + 2, []).append(fin)

        run_pipeline(NB, [st1, st2, st2b, st1b, st3], [0, 1, 2, 0, 3], deferred)

    def dump_att():
        toks = []
        for k in range(2):
            toks.append(p.dma("sp", lambda e, k=k: e.dma_start(out=dbg_att[k * 256:(k + 1) * 256, :], in_=ag2_in[k].ap()),
                              sem=("dbgatt", k)))
        return toks

    def ag2(k):
        p.dma("pool", lambda e, k=k: e.collective_compute("AllGather", ALU.bypass, replica_groups=PAIRS,
                                                          ins=[ag2_in[k].ap().opt()], outs=[ag2_out[k].ap().opt()]),
              writes=[("ag2_out", k)], sem=("cc2", k), inc=1)

    project(0)
    p.dma("pool", lambda e: e.dma_start(out=wqk[:], in_=wqv[:, :, 768:1536]), writes=["wqk"], sem="wqk")
    attn_da()
    p.barrier()
    if stop_after == "da":
        return finish(dump_att())
    ag2(0)
    project(1)
    attn_sb()
    p.barrier()
    if stop_after == "sb":
        return finish(dump_att())
    ag2_tok = {}
    ag2_tok[1] = p.dma("pool", lambda e: e.collective_compute("AllGather", ALU.bypass, replica_groups=PAIRS,
                                                               ins=[ag2_in[1].ap().opt()], outs=[ag2_out[1].ap().opt()]),
                       writes=[("ag2_out", 1)], sem=("cc2", 1), inc=1)

    areset()
    wgate = take([8, 2048], BF16)
    wout = take([8, D], BF16)
    wab = take([2, 4, D], BF16)
    nts = [take([8, 512], BF16) for i in range(2)]
    ah = take([8, 512], BF16)
    am = take([8, 512], BF16)
    gbf = [take([512], BF16) for i in range(16)]
    tmp = [take([512], F32) for i in range(4)]
    yT = take([8, 512], BF16)
    p.dma("pool", lambda e: e.dma_start(out=wgate, in_=wgate_d.rearrange("(c p) m -> p c m", p=128)), writes=["wgate"], sem="wgate")
    p.dma("pool", lambda e: e.dma_start(out=wab[:, 0, :, :], in_=wa_d.rearrange("(c p) m -> p c m", p=128)), writes=["wa"], sem="wa")
    p.dma("pool", lambda e: e.dma_start(out=wab[:, 1, :, :], in_=wb_d.rearrange("(c p) m -> p c m", p=128)), writes=["wb"], sem="wb")
    p.dma("pool", lambda e: e.dma_start(out=wout, in_=wout_d.rearrange("(c p) m -> p c m", p=128)), writes=["wout"], sem="wout")
    ag1own = [ag1_in[k].ap().rearrange("(c p) t -> p c t", p=128) for k in range(2)]
    ag2v = [ag2_out[k].ap().rearrange("(q p) t -> p q t", p=128) for k in range(2)]
    for tt in range(NTT):
        sl = tt % 2
        p.dma("sp", lambda e, tt=tt, sl=sl: e.dma_start(
            out=nts[sl][:], in_=ag1own[tt // 2][:, :, (tt % 2) * 512:(tt % 2 + 1) * 512]),
            writes=[("nts", sl, 0), ("nts", sl, 1)], sem=("nts", sl))
        gi = 0
        for mc in range(8):
            for br in range(2):
                bank = gi % 4
                gi += 1
                col = br * 1024 + mc * 128
                for c in range(8):
                    p.op("pe", lambda e, c=c, col=col, bank=bank, sl=sl: e.matmul(
                        ps[bank][:], lhsT=wgate[:, c, col:col + 128], rhs=nts[sl][:, c, :], start=(c == 0), stop=(c == 7)),
                        reads=["wgate", ("nts", sl, 0), ("nts", sl, 1)], writes=[PS(bank)])
                p.op("act", lambda e, br=br, bank=bank, mc=mc: e.activation(
                    out=gbf[br * 8 + mc][:], in_=ps[bank][:], func=AF.Sigmoid, bias=bgate[:, br * 8 + mc: br * 8 + mc + 1]),
                    reads=[PS(bank), "bgate"], writes=[("gbf", br * 8 + mc)])
        for half in range(2):
            for k in range(2):
                p.dma("sp", lambda e, tt=tt, half=half, k=k: e.dma_start(
                    out=ah[:, k * 4:(k + 1) * 4, :], in_=ag2v[k][:, :, half * T + tt * 512: half * T + (tt + 1) * 512]),
                    reads=([("ag2_out", k)] if (tt == 0 and half == 0) else []), writes=[("ah", k)], sem=("ah", k))
            if half == 0:
                p.op("dve", lambda e: e.tensor_scalar(out=am[:], in0=ah[:], scalar1=sel[:, 0:1], scalar2=None, op0=ALU.mult),
                     reads=[("ah", 0), ("ah", 1), "sel"], writes=["am"])
            else:
                p.op("dve", lambda e: e.scalar_tensor_tensor(out=am[:], in0=ah[:], scalar=sel[:, 1:2], in1=am[:],
                                                             op0=ALU.mult, op1=ALU.add),
                     reads=[("ah", 0), ("ah", 1), "sel", "am"], writes=["am"])
        for mc in range(8):
            par = mc % 2
            for br in range(2):
                bank = 4 + br * 2 + par
                for hh in range(4):
                    k = br * 4 + hh
                    p.op("pe", lambda e, hh=hh, k=k, br=br, bank=bank, mc=mc: e.matmul(
                        ps[bank][:], lhsT=wab[:, br, hh, mc * 128:(mc + 1) * 128], rhs=am[:, k, :], start=(hh == 0), stop=(hh == 3)),
                        reads=["wa" if br == 0 else "wb", "am"], writes=[PS(bank)])
            p.op("dve", lambda e, par=par, mc=mc: e.tensor_tensor(out=tmp[par][:], in0=gbf[mc][:], in1=ps[4 + par][:], op=ALU.mult),
                 reads=[("gbf", mc), PS(4 + par)], writes=[("tmp", par)])
            p.op("dve", lambda e, par=par, mc=mc: e.tensor_tensor(out=tmp[2 + par][:], in0=gbf[8 + mc][:], in1=ps[6 + par][:], op=ALU.mult),
                 reads=[("gbf", 8 + mc), PS(6 + par)], writes=[("tmp", 2 + par)])
            p.op("dve", lambda e, par=par, mc=mc: e.tensor_tensor(out=yT[:, mc, :], in0=tmp[par][:], in1=tmp[2 + par][:], op=ALU.add),
                 reads=[("tmp", par), ("tmp", 2 + par)], writes=[("yT", mc)])
        for tb4 in range(4):
            tb = tt * 4 + tb4
            for mh in range(2):
                bank = (tb4 * 2 + mh) % 2
                for mc in range(8):
                    p.op("pe", lambda e, mc=mc, tb4=tb4, mh=mh, bank=bank: e.matmul(
                        ps[bank][:], lhsT=yT[:, mc, tb4 * 128:(tb4 + 1) * 128], rhs=wout[:, mc, mh * 512:(mh + 1) * 512],
                        start=(mc == 0), stop=(mc == 7)), reads=[("yT", mc), "wout"], writes=[PS(bank)])
                p.op("dve", lambda e, tb=tb, mh=mh, bank=bank: e.tensor_tensor(
                    out=h[:, tb, mh * 512:(mh + 1) * 512], in0=ps[bank][:], in1=h[:, tb, mh * 512:(mh + 1) * 512], op=ALU.add),
                    reads=[PS(bank), ("h", tb)], writes=[("h", tb)])
    p.barrier()
    if stop_after == "merge":
        return finish(dump_h("d"))

    fstate = {}
    out_toks = []

    def pre_final(xn_f):
        fstate["obs"] = [take([D], F32) for i in range(5)]
        fstate["xn"] = xn_f
        p.dma("sp", lambda e: e.dma_start(out=gbuf[:], in_=gains_d[3]), writes=["gbuf"], sem="gbuf")
        p.op("dve", lambda e: e.memset(st[:, 0, :], 0.0), writes=["ss"])

    def final_tb(tb):
        obs = fstate["obs"]
        jb = fstate["xn"][tb % 2]
        sl = tb % 5
        p.op("act", lambda e, tb=tb, jb=jb: e.activation(out=jb[:], in_=h[:, tb, :], func=AF.Square,
                                                         accum_out=st[:, 0, tb:tb + 1]),
             reads=[("h", tb), "ss"], writes=[("ss", tb), ("xn", tb % 2)])
        p.op("dve", lambda e, tb=tb: e.tensor_scalar(out=st[:, 1, tb:tb + 1], in0=st[:, 0, tb:tb + 1], scalar1=1.0 / D, scalar2=EPS,
                                                     op0=ALU.mult, op1=ALU.add), reads=[("ss", tb)], writes=[("ms", tb)])
        p.op("act", lambda e, tb=tb: e.activation(out=st[:, 2, tb:tb + 1], in_=st[:, 1, tb:tb + 1], func=AF.Sqrt),
             reads=[("ms", tb)], writes=[("sd", tb)])
        p.op("dve", lambda e, tb=tb: e.reciprocal(out=st[:, 3, tb:tb + 1], in_=st[:, 2, tb:tb + 1]), reads=[("sd", tb)], writes=[("rstd", tb)])
        p.op("dve", lambda e, tb=tb, sl=sl: e.scalar_tensor_tensor(
            out=obs[sl], in0=h[:, tb, :], scalar=st[:, 3, tb:tb + 1], in1=gbuf[:], op0=ALU.mult, op1=ALU.mult),
            reads=[("h", tb), ("rstd", tb), "gbuf"], writes=[("ob", sl)])
        out_toks.append(p.dma("sp", lambda e, tb=tb, sl=sl: e.dma_start(out=out_d[tb * 128:(tb + 1) * 128, :], in_=obs[sl]),
                              reads=[("ob", sl)], sem=("ob", sl)))

    if stop_after == "ffn2":
        ffn(1)
        return finish(dump_h("e"))
    ffn(1, on_final=final_tb, pre_final=pre_final)
    p.wait_all("sp", out_toks)
    p.wait_all("act", out_toks)
    p.emit(nc)
    return nc


def _consts():
    bf = ml_dtypes.bfloat16
    i = np.arange(128)
    cb = np.zeros((128, 6, 128), np.float32)
    cb[:, 0, :] = np.eye(128)
    cb[:, 1, :] = (i[:, None] >= i[None, :])
    cb[:, 2, :] = 1.0
    cb[:, 3, :] = (i[:, None] <= i[None, :])
    cb[:, 4, :] = (i[:, None] < i[None, :])
    return cb.astype(bf)


def make_in_maps(inputs):
    f = lambda a: np.ascontiguousarray(np.asarray(a, dtype=np.float32))
    x = f(inputs["x"])
    gains = np.stack([np.broadcast_to(f(inputs[k]).reshape(-1), (128, D)) for k in
                      ("ffn1_norm", "mix_norm", "ffn2_norm", "final_norm")]).astype(np.float32)
    w_in = f(inputs["w_in"])[0]
    cbf = _consts()
    lamv = np.stack([np.broadcast_to(f(inputs[k])[0], (128, 64)) for k in
                     ("lambda_q1", "lambda_k1", "lambda_q2", "lambda_k2")], axis=1).astype(np.float32)
    subln = f(inputs["diff_subln"])[0].reshape(128, 1)
    bg = f(inputs["b_gate"])[0].reshape(16, 128).T.copy()
    slopes = 2.0 ** (-8.0 * np.arange(1, 5) / 4.0)
    maps = []
    shared = dict(
        gains=gains, wg1=f(inputs["ffn1_w_gate"])[0], wu1=f(inputs["ffn1_w_up"])[0], wd1=f(inputs["ffn1_w_down"])[0],
        wg2=f(inputs["ffn2_w_gate"])[0], wu2=f(inputs["ffn2_w_up"])[0], wd2=f(inputs["ffn2_w_down"])[0],
        wgate=np.ascontiguousarray(w_in[:, 3072:5120]), bgate=bg, wa=f(inputs["w_branch_diff"])[0],
        wb=f(inputs["w_branch_sb"])[0], wout=f(inputs["w_out"])[0], lamv=lamv, subln=subln, cbf=cbf)
    pidx = np.arange(128, dtype=np.float64)[:, None]
    jj = np.arange(-28, 4, dtype=np.float64)[None, :]
    for c in range(8):
        b, r = c // 2, c % 2
        cols = []
        for base in (0, 512, 1024, 1536, 2048, 2560):
            cols.append(w_in[:, base + r * 256: base + (r + 1) * 256])
        wqkv = np.ascontiguousarray(np.concatenate(cols, axis=1))
        ab = np.stack([slopes[2 * r + hl] * (pidx + 128.0 * jj - 256.0) for hl in range(2)], axis=1).astype(np.float32)
        sel = np.zeros((128, 2), np.float32)
        sel[:, r] = 1.0
        m = dict(shared)
        m.update(x=np.ascontiguousarray(x[b, r * T:(r + 1) * T, :]), wqkv=wqkv, abias=ab, sel=sel)
        maps.append(m)
    return maps


_CACHE = {}


def kernel(**inputs):
    if "nc" not in _CACHE:
        _CACHE["nc"] = build_program()
    nc = _CACHE["nc"]
    maps = make_in_maps(inputs)
    res = run_bass_kernel_spmd(nc, maps, core_ids=list(range(8)))
    out = np.empty((4, S, D), np.float32)
    for c in range(8):
        b, r = c // 2, c % 2
        out[b, r * T:(r + 1) * T, :] = res.results[c]["out"]
    return out
```
